# Optimizing a Trainium2 kernel written in Bass

```python
import math
import jax, jax.numpy as jnp
from jax import lax
import numpy as np

D_MODEL = 2048
BATCH = 4
SEQ = 2048
DEPTH = 1
DEC_BATCH = 128
DEC_SEQ = 1
PAST_LEN = 16384
PAGE_SIZE = 128

D_MIX = D_MODEL
D_A = D_MIX // 2
D_CONV = D_MIX - D_A
N_HEADS_A = 4
HEAD_V = D_A // N_HEADS_A
D_K = D_A // 2
HEAD_K = D_K // N_HEADS_A
GATE_RANK = 16
GATE_NORM = 16.0
CHUNK = 16
CONV_W = 31
CONV_GROUPS = 8
D_FF = ((8 * D_MODEL // 3 + 255) // 256) * 256
IN_SIZES = (D_K, D_K, D_A, GATE_RANK, D_A, 2 * D_CONV)
IN_COLS = sum(IN_SIZES)
IN_SPLITS = tuple(int(s) for s in np.cumsum(IN_SIZES)[:-1])
RMS_EPS = 1e-6
LN_EPS = 1e-5

kernel_name = "hymba_gla_conformerconv_decode_step"


def rmsnorm(x, g):
    xf = x.astype(jnp.float32)
    y = xf * lax.rsqrt(jnp.mean(xf * xf, axis=-1, keepdims=True) + RMS_EPS)
    return (y * g.astype(jnp.float32)).astype(x.dtype)


def layernorm(x, g, b):
    xf = x.astype(jnp.float32)
    mu = jnp.mean(xf, axis=-1, keepdims=True)
    xc = xf - mu
    y = xc * lax.rsqrt(jnp.mean(xc * xc, axis=-1, keepdims=True) + LN_EPS)
    return (y * g.astype(jnp.float32) + b.astype(jnp.float32)).astype(x.dtype)


def gla_recurrence(q, k, v, log_a, s0):
    B, T, H, DK = q.shape
    DV = v.shape[-1]
    n = -(-T // CHUNK)
    pad = n * CHUNK - T
    padf = lambda a: jnp.pad(a.astype(jnp.float32), ((0, 0), (0, pad), (0, 0), (0, 0)))
    to_chunks = lambda a: padf(a).reshape(B, n, CHUNK, H, a.shape[-1]).transpose(1, 0, 3, 2, 4)
    qc, kc, vc, gc = to_chunks(q), to_chunks(k), to_chunks(v), to_chunks(log_a)
    causal = jnp.tril(jnp.ones((CHUNK, CHUNK), dtype=bool))[:, :, None]

    def step(S, inp):
        qi, ki, vi, gi = inp
        b = jnp.cumsum(gi, axis=-2)
        diff = b[..., :, None, :] - b[..., None, :, :]
        decay = jnp.exp(jnp.where(causal, diff, -jnp.inf))
        A = jnp.einsum('bhid,bhjd,bhijd->bhij', qi, ki, decay)
        o = jnp.einsum('bhij,bhjv->bhiv', A, vi) + jnp.einsum('bhid,bhdv->bhiv', qi * jnp.exp(b), S)
        b_last = b[..., -1:, :]
        S = jnp.exp(b_last[..., 0, :])[..., None] * S + jnp.einsum(
            'bhjd,bhjv->bhdv', ki * jnp.exp(b_last - b), vi)
        return S, o

    S, o = lax.scan(step, s0.astype(jnp.float32), (qc, kc, vc, gc))
    o = o.transpose(1, 0, 3, 2, 4).reshape(B, n * CHUNK, H, DV)[:, :T]
    return o, S


def causal_depthwise_conv(u, buf, w, b):
    full = jnp.concatenate([buf.astype(u.dtype), u], axis=1)
    y = lax.conv_general_dilated(
        full, w.astype(u.dtype)[:, None, :], window_strides=(1,), padding='VALID',
        dimension_numbers=('NWC', 'WIO', 'NWC'), feature_group_count=u.shape[-1])
    return y + b.astype(u.dtype), full[:, -(CONV_W - 1):]


def hybrid_layer(x, s_gla, s_conv, norm_mix, w_in, w_gate_up, b_gate, gla_norm,
                 conv_w, conv_b, conv_ln_g, conv_ln_b, w_out, norm_ffn, w_ffn_in, w_ffn_out):
    B, T, _ = x.shape
    h = rmsnorm(x, norm_mix)
    z = h @ w_in
    q, k, v, g_lr, g_out, u = jnp.split(z, IN_SPLITS, axis=-1)
    q = q.reshape(B, T, N_HEADS_A, HEAD_K) * (HEAD_K ** -0.5)
    k = k.reshape(B, T, N_HEADS_A, HEAD_K)
    v = v.reshape(B, T, N_HEADS_A, HEAD_V)
    gate_logit = (g_lr @ w_gate_up + b_gate).astype(jnp.float32)
    log_a = (jax.nn.log_sigmoid(gate_logit) / GATE_NORM).reshape(B, T, N_HEADS_A, HEAD_K)
    o, S_new = gla_recurrence(q, k, v, log_a, s_gla)
    o = rmsnorm(o, gla_norm) * jax.nn.silu(g_out.reshape(B, T, N_HEADS_A, HEAD_V).astype(jnp.float32))
    o_a = o.reshape(B, T, D_A).astype(x.dtype)
    ua, ug = jnp.split(u, 2, axis=-1)
    glu = ua * jax.nn.sigmoid(ug)
    c, conv_buf_new = causal_depthwise_conv(glu, s_conv, conv_w, conv_b)
    c = jax.nn.silu(layernorm(c, conv_ln_g, conv_ln_b))
    x = x + jnp.concatenate([o_a, c], axis=-1) @ w_out
    hf = rmsnorm(x, norm_ffn) @ w_ffn_in
    f_gate, f_up = jnp.split(hf, 2, axis=-1)
    x = x + (jax.nn.silu(f_gate) * f_up) @ w_ffn_out
    return x, S_new.astype(x.dtype), conv_buf_new.astype(x.dtype)


def setup_inputs(seed: int = 0) -> dict:
    key = jax.random.key(seed)
    ks = jax.random.split(key, 20)
    nrm = lambda k, shape, s: jax.random.normal(k, shape, jnp.float32) * s
    return {
        "x_prompt": nrm(ks[0], (BATCH, SEQ, D_MODEL), 1.0),
        "x_sample": nrm(ks[1], (DEC_BATCH, DEC_SEQ, D_MODEL), 1.0),
        "state_gla": nrm(ks[2], (DEPTH, DEC_BATCH, N_HEADS_A, HEAD_K, HEAD_V), 0.5),
        "state_conv": nrm(ks[3], (DEPTH, DEC_BATCH, CONV_W - 1, D_CONV), 0.5),
        "norm_mix": 1.0 + nrm(ks[4], (DEPTH, D_MODEL), 0.02),
        "w_in": nrm(ks[5], (DEPTH, D_MODEL, IN_COLS), D_MODEL ** -0.5),
        "w_gate_up": nrm(ks[6], (DEPTH, GATE_RANK, D_K), GATE_RANK ** -0.5),
        "b_gate": nrm(ks[7], (DEPTH, D_K), 0.1),
        "gla_norm": 1.0 + nrm(ks[8], (DEPTH, HEAD_V), 0.02),
        "conv_w": nrm(ks[9], (DEPTH, CONV_W, D_CONV), CONV_W ** -0.5),
        "conv_b": nrm(ks[10], (DEPTH, D_CONV), 0.01),
        "conv_ln_g": 1.0 + nrm(ks[11], (DEPTH, D_CONV), 0.02),
        "conv_ln_b": nrm(ks[12], (DEPTH, D_CONV), 0.01),
        "w_out": nrm(ks[13], (DEPTH, D_MIX, D_MODEL), D_MIX ** -0.5),
        "norm_ffn": 1.0 + nrm(ks[14], (DEPTH, D_MODEL), 0.02),
        "w_ffn_in": nrm(ks[15], (DEPTH, D_MODEL, 2 * D_FF), D_MODEL ** -0.5),
        "w_ffn_out": nrm(ks[16], (DEPTH, D_FF, D_MODEL), D_FF ** -0.5),
        "norm_final": 1.0 + nrm(ks[17], (D_MODEL,), 0.02),
    }


def reference(x_prompt, x_sample, state_gla, state_conv, norm_mix, w_in, w_gate_up, b_gate,
              gla_norm, conv_w, conv_b, conv_ln_g, conv_ln_b, w_out, norm_ffn, w_ffn_in,
              w_ffn_out, norm_final):
    yp, ys = x_prompt, x_sample
    bp = x_prompt.shape[0]
    gla_p, conv_p, gla_s, conv_s = [], [], [], []
    for l in range(DEPTH):
        params = (norm_mix[l], w_in[l], w_gate_up[l], b_gate[l], gla_norm[l], conv_w[l], conv_b[l],
                  conv_ln_g[l], conv_ln_b[l], w_out[l], norm_ffn[l], w_ffn_in[l], w_ffn_out[l])
        s0 = jnp.zeros((bp, N_HEADS_A, HEAD_K, HEAD_V), jnp.float32)
        c0 = jnp.zeros((bp, CONV_W - 1, D_CONV), x_prompt.dtype)
        yp, sp, cp = hybrid_layer(yp, s0, c0, *params)
        ys, ss, cs = hybrid_layer(ys, state_gla[l], state_conv[l], *params)
        gla_p.append(sp)
        conv_p.append(cp)
        gla_s.append(ss)
        conv_s.append(cs)
    yp = rmsnorm(yp, norm_final)
    ys = rmsnorm(ys, norm_final)
    return (yp, ys, jnp.stack(gla_p), jnp.stack(conv_p), jnp.stack(gla_s), jnp.stack(conv_s))
```

```python
import contextlib
import numpy as np
import concourse.bass as bass
import concourse.mybir as mybir
from concourse.bass_utils import run_bass_kernel_spmd

F32 = mybir.dt.float32
BF16 = mybir.dt.bfloat16
AF = mybir.ActivationFunctionType
ALU = mybir.AluOpType
AX = mybir.AxisListType

D = 2048
KC = 16
NPR = 1024
NS = 16
NH = 32
NTA = NPR + NS + NH
NT = NPR + NS
NPRE = 1024
DFF = 5632
NJ = DFF // 128
C_Q, C_K, C_V, C_GLR, C_GOUT, C_UA, C_UG = 0, 512, 1024, 2048, 2064, 3088, 4112
ARENA_WORDS = 47616


def _split(n0, n, parts):
    out = []
    base = n // parts
    rem = n % parts
    o = n0
    for i in range(parts):
        s = base + (1 if i < rem else 0)
        out.append((o, s))
        o += s
    return out


BLK_A = _split(0, NTA, 3)
BLK_PRE = [(0, 512), (512, 512)]
BLK_C = _split(0, NT, 3)


class Prog:
    ENG = ("pe", "act", "dve", "pool", "sp")

    def __init__(self, nc):
        self.nc = nc
        self.lists = {e: [] for e in self.ENG}
        self.cnt = {e: 0 for e in self.ENG}
        self.waited = {e: {} for e in self.ENG}
        self.lastw = {}
        self.reads = {}
        self.dma_cnt = {}
        self.final_tokens = []
        self.bufs = {}
        self.bufsum = {}
        self.overl = {}
        self.phase = 0
        self.nops = 0

    def plan(self, specs, cap):
        orders = [sorted(specs, key=lambda s: (-(s[3] - s[2]), -s[1])), sorted(specs, key=lambda s: -s[1]),
                  sorted(specs, key=lambda s: (s[2], -s[1])), sorted(specs, key=lambda s: (-s[1] * (s[3] - s[2] + 1)))]
        rng = np.random.default_rng(1234)
        for _ in range(300):
            keys = {s[0]: -s[1] * (s[3] - s[2] + 1) * float(rng.uniform(0.3, 1.0)) for s in specs}
            orders.append(sorted(specs, key=lambda s: keys[s[0]]))
        best = None
        for order in orders:
            placed = []
            ok = True
            for (name, words, f, l) in order:
                words = (words + 7) // 8 * 8
                cands = sorted([(p[1], p[2]) for p in placed if not (p[4] < f or l < p[3])])
                lo = 0
                for (plo, phi) in cands:
                    if lo + words <= plo:
                        break
                    lo = max(lo, phi)
                placed.append((name, lo, lo + words, f, l))
            top = max(p[2] for p in placed)
            if best is None or top < best[0]:
                best = (top, placed)
        top, placed = best
        assert top <= cap, f"arena overflow: {top} > {cap}"
        for (name, lo, hi, f, l) in placed:
            self.bufs[name] = (lo, hi, f, l)
            self.bufsum[name] = {}
        for a in placed:
            self.overl[a[0]] = [b[0] for b in placed if b[0] != a[0] and not (b[2] <= a[1] or a[2] <= b[1])]
        return max(p[2] for p in placed)

    def _bufname(self, k):
        n = k[0] if isinstance(k, tuple) else k
        return n if n in self.bufs else None

    def _deps(self, eng, reads, writes):
        deps = {}

        def add(t):
            if deps.get(t[0], 0) < t[1]:
                deps[t[0]] = t[1]
        touched = set()
        for k in reads:
            if k in self.lastw:
                add(self.lastw[k])
            b = self._bufname(k)
            if b:
                touched.add(b)
        for k in writes:
            if k in self.lastw:
                add(self.lastw[k])
            for t in self.reads.get(k, ()):
                add(t)
            b = self._bufname(k)
            if b:
                touched.add(b)
        for b in touched:
            lo, hi, f, l = self.bufs[b]
            assert f <= self.phase <= l, f"buffer {b} used in phase {self.phase}, declared [{f},{l}]"
            for o in self.overl[b]:
                for s, v in self.bufsum[o].items():
                    add((s, v))
        waits = []
        for s in sorted(deps):
            v = deps[s]
            if eng == "pe" and s == "c_pe":
                continue
            if self.waited[eng].get(s, 0) < v:
                self.waited[eng][s] = v
                waits.append((s, v))
        return waits, touched

    def _commit(self, tok, reads, writes, touched):
        for k in reads:
            self.reads.setdefault(k, set()).add(tok)
        for k in writes:
            self.lastw[k] = tok
            self.reads[k] = set()
        for b in touched:
            d = self.bufsum[b]
            if d.get(tok[0], 0) < tok[1]:
                d[tok[0]] = tok[1]

    def op(self, eng, fn, reads=(), writes=(), sig=True):
        waits, touched = self._deps(eng, reads, writes)
        if sig:
            self.cnt[eng] += 1
            tok = ("c_" + eng, self.cnt[eng])
        else:
            tok = ("c_" + eng, self.cnt[eng] + 1)
        self._commit(tok, reads, writes, touched)
        self.lists[eng].append((waits, fn, ("c_" + eng, 1) if sig else None))
        self.nops += 1
        return tok

    def dma(self, q, sem, out, in_, reads=(), writes=(), final=False, **kw):
        waits, touched = self._deps(q, reads, writes)
        self.dma_cnt[sem] = self.dma_cnt.get(sem, 0) + 1
        tok = (sem, 16 * self.dma_cnt[sem])
        self._commit(tok, reads, writes, touched)
        self.lists[q].append((waits, (lambda e: e.dma_start(out=out, in_=in_, **kw)), (sem, 16)))
        if final:
            self.final_tokens.append(tok)
        self.nops += 1
        return tok

    def wait_tokens(self, eng, toks):
        waits = []
        best = {}
        for (s, v) in toks:
            best[s] = max(best.get(s, 0), v)
        for s in sorted(best):
            v = best[s]
            if self.waited[eng].get(s, 0) < v:
                self.waited[eng][s] = v
                waits.append((s, v))
        if waits:
            self.lists[eng].append((waits, None, None))

    def emit(self):
        nc = self.nc
        semnames = set("c_" + e for e in self.ENG) | set(self.dma_cnt)
        with contextlib.ExitStack() as st:
            sems = {n: st.enter_context(nc.semaphore(n)) for n in sorted(semnames)}
            block = st.enter_context(nc.Block())
            handles = {"pe": block.tensor, "act": block.scalar, "dve": block.vector,
                       "pool": block.gpsimd, "sp": block.sync}

            def make(ename):
                items = self.lists[ename]

                def body(e):
                    for waits, fn, sig in items:
                        for (s, v) in waits:
                            e.wait_ge(sems[s], v)
                        if fn is None:
                            continue
                        ins = fn(e)
                        if sig is not None:
                            ins.then_inc(sems[sig[0]], sig[1])
                return body

            for ename in self.ENG:
                if self.lists[ename]:
                    handles[ename](make(ename))


def build_program(stage=99, taps=()):
    nc = bass.Bass("TRN2", target_bir_lowering=False)
    P = Prog(nc)
    taps = set(taps)

    def din(name, shape):
        return nc.dram_tensor(name, list(shape), F32, kind="ExternalInput").ap()

    def dout(name, shape):
        return nc.dram_tensor(name, list(shape), F32, kind="ExternalOutput").ap()

    xp = din("xp", [NPR, D]); xsm = din("xsm", [NS, D]); xpre = din("xpre", [NPRE, D])
    sgla = din("sgla", [NS, 4, 128, 256]); sconv = din("sconv", [NS * 30, 1024])
    norm_mix = din("norm_mix", [D]); w_in = din("w_in", [D, 5136]); w_gate_up = din("w_gate_up", [16, 512])
    b_gate = din("b_gate", [512]); gla_norm = din("gla_norm", [256]); conv_w = din("conv_w", [31, 1024])
    conv_b = din("conv_b", [1024]); conv_ln_g = din("conv_ln_g", [1024]); conv_ln_b = din("conv_ln_b", [1024])
    w_out = din("w_out", [D, D]); norm_ffn = din("norm_ffn", [D]); w_ffn_in = din("w_ffn_in", [D, 2 * DFF])
    w_ffn_out = din("w_ffn_out", [DFF, D]); norm_final = din("norm_final", [D])
    c_ident = din("c_ident", [128, 128]); c_mask = din("c_mask", [128, 128])
    c_reset = din("c_reset", [128, NTA]); c_delta = din("c_delta", [128, 256])
    yp = dout("yp", [NPR, D]); ys = dout("ys", [NS, D]); glap = dout("glap", [4, 128, 256])
    convp = dout("convp", [30, 1024]); glas = dout("glas", [NS, 4, 128, 256]); convs = dout("convs", [NS * 30, 1024])
    x1d = nc.dram_tensor("x1d", [NT, D], F32, kind="Internal").ap()
    NCVT = 4
    wbf = nc.dram_tensor("wbf", [NCVT, 4, 128, 11 * 256], BF16, kind="Internal").ap()
    tap_out = {}

    LASTP = 11
    specs = [
        ("identf", 128, 0, LASTP), ("identb", 64, 0, LASTP), ("mask", 128, 0, LASTP), ("onesb", 64, 0, LASTP),
        ("reset", NTA, 0, 5), ("delta", 256, 0, 8), ("vecs", 64, 0, LASTP), ("wgb", 256, 0, 5),
        ("wT", 256, 0, 8), ("small", 256, 0, LASTP),
        ("gb", 2048, 0, 3), ("gbB", 2048, 9, 9), ("gbD", 2048, 11, 11),
        ("xsA", 4096, 0, 3), ("xnA", 2048, 0, 3),
        ("xsB", 4096, 9, 9), ("xnB", 2048, 9, 9),
        ("ws", 4096, 2, 7), ("wsC", 4096, 10, 10),
        ("hTpre", 8192, 1, 3), ("hT", KC * NTA // 2, 2, 7),
        ("S", 1024, 2, 5), ("Sb", 8 * 128, 2, 5),
        ("eTp", NPRE, 2, 2), ("lTp", NPRE, 2, 2), ("csp", NPRE, 2, 2), ("mq1p", 8, 2, 2), ("mkp", NPRE, 2, 2),
        ("nblp", 16, 2, 2), ("glrTp", NPRE // 2, 2, 2), ("kTpp", 4 * NPRE // 2, 2, 2), ("vTp", 2 * NPRE // 2, 2, 2), ("Vp", 4096, 2, 2),
        ("eT", NTA, 4, 4), ("lT", NTA, 4, 4), ("cs", NTA, 4, 4), ("mq1", NTA, 4, 4), ("mq2", NTA, 4, 4), ("mk", NTA, 4, 4),
        ("nbl", 16, 4, 4), ("ebl", 64, 2, 5),
        ("glrT", NTA // 2, 4, 4),
        ("qT1", 4 * NTA // 2, 4, 5), ("qT2", 4 * NTA // 2, 4, 5), ("kTp", 4 * NTA // 2, 4, 5),
        ("vT", 2 * NTA // 2, 4, 4), ("V", 4096, 4, 5), ("Vs", 512, 4, 6),
        ("sg", 8 * NT // 2, 4, 6),
        ("mixT", KC * NT // 2, 5, 9),
        ("ATm", 8 * 64, 5, 5), ("Ktok", 8 * 64, 2, 5), ("on", 2 * 512, 5, 6), ("ssg", 64, 5, 6),
        ("stash", 256, 4, 6),
        ("Ssm", 2 * 4096, 6, 6), ("Ssb", 4 * 128, 6, 6), ("Vexp", 2 * 2048, 6, 6), ("qexp", 512, 6, 6), ("ktoks", 256, 6, 6),
        ("sgm", 2 * NTA, 7, 7), ("gl32", 2 * NTA, 7, 7), ("fullg", 8 * 1056 // 2, 7, 8),
        ("glus", 128, 7, 8), ("glul", 256, 7, 8),
        ("Dm", 16 * 64, 8, 8), ("y32", 8192, 8, 8), ("ybf", 4 * 256, 7, 8), ("ysq", 4 * 256, 7, 8),
        ("lnt", 4 * 512, 7, 8), ("lnd", 2 * 512, 7, 8),
        ("sctok", 1024, 7, 7), ("scT", 3840, 7, 7), ("ys32", 3 * 128, 7, 7),
        ("WoA", 4096, 8, 9), ("WoB", 4096, 8, 9), ("WoC", 4096, 8, 9), ("WoD", 4096, 9, 9),
        ("h2T", KC * NT // 2, 9, 10),
        ("actT", NJ * NT // 2, 10, 11), ("sgt", 6 * 352, 10, 10),
        ("wsD", 2 * 2816, 10, 11),
        ("x2", 5 * 2048, 11, 11), ("yst", 4 * 528, 11, 11), ("junkD", 1024, 11, 11),
    ]
    top = P.plan(specs, ARENA_WORDS)

    st = contextlib.ExitStack()
    arena = st.enter_context(nc.sbuf_tensor("arena", [128, ARENA_WORDS], F32))
    banks = [st.enter_context(nc.psum_tensor(f"bank{i}", [128, 512], F32)) for i in range(8)]

    def A(name, dt=F32, off=0, n=None):
        lo, hi, f, l = P.bufs[name]
        a = arena[:, lo + off: (hi if n is None else lo + off + n)]
        return a if dt == F32 else a.bitcast(BF16)

    def PSF(b):
        return banks[b][:, :]

    def PSB(b):
        return banks[b][:, :].bitcast(BF16)

    def tap(name, ap, shape):
        if name not in taps:
            return
        t = dout("tap_" + name, shape)
        tap_out[name] = shape
        q = "pool" if ap.dtype == BF16 else "sp"
        P.wait_tokens(q, list(P.lastw.values()))
        P.dma(q, "d_tap_" + name, t, ap, final=True)

    def tap_bf(name, ap_bf, p, n):
        if name not in taps:
            return
        raise NotImplementedError

    identf = A("identf"); identb = A("identb", BF16); maskf = A("mask"); onesb = A("onesb", BF16)
    resetm = A("reset"); deltar = A("delta"); vecs = A("vecs"); small = A("small"); gb = A("gb")
    wgb = A("wgb", BF16)
    wT = A("wT")
    P.phase = 0
    P.dma("sp", "d_c0", identf, c_ident[:, :], writes=["identf"])
    P.dma("sp", "d_c1", maskf, c_mask[:, :], writes=["mask"])
    P.dma("sp", "d_c2", resetm, c_reset[:, :], writes=["reset"])
    P.dma("sp", "d_c3", deltar, c_delta[:, :], writes=["delta"])
    for (sem_, c0_, c1_, src_, key_) in (("d_c4", 32, 36, b_gate, "bg"), ("d_c5", 4, 6, gla_norm, "gn"), ("d_c6", 8, 16, conv_b, "cb"),
                                       ("d_c7", 16, 24, conv_ln_g, "lg"), ("d_c8", 24, 32, conv_ln_b, "lb")):
        P.dma("sp", sem_, vecs[:, c0_:c1_], src_.rearrange("(h p) -> p h", p=128), writes=[("vecs", key_)],
              allow_slow_non_contiguous=True)
    P.dma("pool", "d_c9", wgb[0:16, 0:512], w_gate_up[:, :], writes=["wgb"])
    P.op("dve", lambda e: e.tensor_copy(out=identb, in_=identf), reads=["identf"], writes=["identb"])
    P.op("dve", lambda e: e.memset(onesb, 1.0), writes=["onesb"])
    P.op("dve", lambda e: e.tensor_scalar(out=vecs[:, 0:4], in0=vecs[:, 32:36], scalar1=-1.0, scalar2=None, op0=ALU.mult),
         reads=[("vecs", "bg")], writes=[("vecs", "nbg")])

    def cvt_dmas():
        for mp in range(NCVT):
            for kq in range(4):
                P.dma("pool", f"d_cvt{mp}_{kq}", wbf[mp, kq].rearrange("p (j c) -> p j c", j=11),
                      w_ffn_out[kq * 1408:(kq + 1) * 1408, mp * 256:(mp + 1) * 256].rearrange("(j p) c -> p j c", p=128),
                      writes=[("wbf", mp, kq)])

    ps_rr = [0]

    def psk(b):
        return [("ps", b, 0), ("ps", b, 1)]

    def psh(b, half):
        return [("ps", b, half)]

    def next_bank(pool):
        b = pool[ps_rr[0] % len(pool)]
        ps_rr[0] += 1
        return b

    ws_rr = [0]

    def load_gain(src, sem, gname="gb"):
        P.dma("sp", sem, A(gname), src.partition_broadcast(128), writes=[gname])

    def norm_transpose(tiles, xs_name, xn_name, dstT, dst_name, ntok_dst, tbanks, store_x1=None, resid=None, gname="gb", as_gen=False):
        xs = [A(xs_name, F32, 0, 2048), A(xs_name, F32, 2048, 2048)]
        xn = [A(xn_name, BF16)[:, 0:2048], A(xn_name, BF16)[:, 2048:4096]]
        dst3 = dstT.rearrange("p (k t) -> p k t", k=KC)
        def part1(i):
            src, r, col = tiles[i]
            s = i % 2
            kx = (xs_name, s); kn = (xn_name, s); kss = ("small", "nss", s)
            if resid is None:
                P.dma("sp", f"d_{xs_name}{s}", xs[s][0:r, :], src, writes=[kx])
            else:
                resid(i, xs[s], kx, r)
            sscol = small[:, 4 * s: 4 * s + 1]
            P.op("act", lambda e, s=s, r=r, sscol=sscol: e.activation(out=xn[s][0:r, :], in_=xs[s][0:r, :], func=AF.Square,
                                                                      accum_out=sscol[0:r, :]),
                 reads=[kx], writes=[kn, kss])
            P.op("act", lambda e, s=s, r=r: e.activation(out=small[0:r, 4 * s + 1: 4 * s + 2], in_=small[0:r, 4 * s: 4 * s + 1],
                                                         func=AF.Sqrt, scale=1.0 / D, bias=1e-6),
                 reads=[kss], writes=[("small", "nsd", s)])
            P.op("dve", lambda e, s=s, r=r: e.reciprocal(out=small[0:r, 4 * s + 2: 4 * s + 3], in_=small[0:r, 4 * s + 1: 4 * s + 2]),
                 reads=[("small", "nsd", s)], writes=[("small", "nrs", s)])
            P.op("dve", lambda e, s=s, r=r: e.scalar_tensor_tensor(out=xn[s][0:r, :], in0=xs[s][0:r, :],
                                                                   scalar=small[0:r, 4 * s + 2: 4 * s + 3], in1=A(gname)[0:r, :],
                                                                   op0=ALU.mult, op1=ALU.mult),
                 reads=[kx, ("small", "nrs", s), gname], writes=[kn])
            if store_x1 is not None:
                store_x1(i, xs[s], kx, r)

        def part2(i):
            src, r, col = tiles[i]
            s = i % 2
            kn = (xn_name, s)
            for half in range(2):
                b = tbanks[half]
                pb = PSB(b)
                for k8 in range(8):
                    kc = half * 8 + k8
                    P.op("pe", lambda e, s=s, r=r, kc=kc, k8=k8, pb=pb: e.transpose(
                        out=pb[:, k8 * 128: k8 * 128 + r], in_=xn[s][0:r, kc * 128:(kc + 1) * 128], identity=identb[0:r, 0:r]),
                        reads=[kn, "identb"], writes=psk(b), sig=(k8 == 7))
                src3 = pb.rearrange("p (k t) -> p k t", k=8)[:, :, 0:r]
                dsto = dst3[:, half * 8:(half + 1) * 8, col:col + r]
                if half == 0:
                    P.op("act", lambda e, src3=src3, dsto=dsto: e.activation(out=dsto, in_=src3, func=AF.Copy),
                         reads=psk(b), writes=[(dst_name, i)])
                else:
                    P.op("dve", lambda e, src3=src3, dsto=dsto: e.tensor_copy(out=dsto, in_=src3),
                         reads=psk(b), writes=[(dst_name, i)])

        n_t = len(tiles)

        def gen():
            part1(0)
            yield
            for i in range(n_t):
                if i + 1 < n_t:
                    part1(i + 1)
                    yield
                part2(i)
                yield
        if as_gen:
            return gen()
        for _ in gen():
            pass

    bg_gens = []

    def tick():
        while bg_gens:
            try:
                next(bg_gens[0])
                return
            except StopIteration:
                bg_gens.pop(0)

    def drain():
        while bg_gens:
            tick()

    def proj(wsrc, c0, M, srcT, src_name, src_tiles, ntok, blocks, consumer, accbanks, wkey="ws"):
        hs = [ws_rr[0] % 8, (ws_rr[0] + 1) % 8]
        ws_rr[0] += 2
        wview = A(wkey, BF16).rearrange("p (s k c) -> p s k c", s=8, k=8)
        wsrc3 = wsrc[:, c0:c0 + M].rearrange("(k p) c -> p k c", p=128)
        for hh in range(2):
            P.dma("pool", f"d_ws{hs[hh]}", wview[:, hs[hh], :, 0:M], wsrc3[:, hh * 8:(hh + 1) * 8, :], writes=[(wkey, hs[hh])])
        src3 = srcT.rearrange("p (k t) -> p k t", k=KC)
        for bi, (b0, n) in enumerate(blocks):
            b = next_bank(accbanks)
            for kc in range(KC):
                P.op("pe", lambda e, b=b, kc=kc, b0=b0, n=n: e.matmul(PSF(b)[0:M, 0:n], lhsT=wview[:, hs[kc // 8], kc % 8, 0:M],
                                                                       rhs=src3[:, kc, b0:b0 + n], start=(kc == 0), stop=(kc == KC - 1)),
                     reads=[(wkey, hs[kc // 8])] + src_tiles(b0, n), writes=psk(b), sig=(kc == KC - 1))
            consumer(bi, b, b0, n)
            tick()

    def proj2(wsrc, c0, srcT, src_tiles, blocks, consumers, accbanks, wkey="ws"):
        qs = [(ws_rr[0] + q) % 8 for q in range(4)]
        ws_rr[0] += 4
        wview = A(wkey, BF16).rearrange("p (s k c) -> p s k c", s=8, k=4)
        for q in range(4):
            P.dma("pool", f"d_ws{qs[q]}", wview[:, qs[q], :, :],
                  wsrc[q * 512:(q + 1) * 512, c0:c0 + 256].rearrange("(k p) c -> p k c", p=128), writes=[(wkey, qs[q])])
        src3 = srcT.rearrange("p (k t) -> p k t", k=KC)
        for mt in range(2):
            for bi, (b0, n) in enumerate(blocks):
                b = next_bank(accbanks)
                for kc in range(KC):
                    P.op("pe", lambda e, b=b, kc=kc, b0=b0, n=n, mt=mt: e.matmul(
                        PSF(b)[:, 0:n], lhsT=wview[:, qs[kc // 4], kc % 4, mt * 128:(mt + 1) * 128],
                        rhs=src3[:, kc, b0:b0 + n], start=(kc == 0), stop=(kc == KC - 1)),
                        reads=[(wkey, qs[kc // 4])] + src_tiles(b0, n), writes=psk(b), sig=(kc == KC - 1))
                consumers[mt](bi, b, b0, n)
                tick()

    def tiles_of(name, tile_cols):
        def f(b0, n):
            return [(name, i) for (i, c0, c1) in tile_cols if c0 < b0 + n and b0 < c1]
        return f

    hTpre = A("hTpre", BF16)
    hT = A("hT", BF16)
    hT3 = hT.rearrange("p (k t) -> p k t", k=KC)
    hTpre3 = hTpre.rearrange("p (k t) -> p k t", k=KC)
    P.phase = 1
    load_gain(norm_mix, "d_gb")
    pre_tiles = [(xpre[t * 128:(t + 1) * 128, :], 128, t * 128) for t in range(8)]
    norm_transpose(pre_tiles, "xsA", "xnA", hTpre, "hTpre", NPRE, (6, 7))
    pre_tilecols = [(t, t * 128, (t + 1) * 128) for t in range(8)]
    pre_src = tiles_of("hTpre", pre_tilecols)
    main_tilecols = [(t, t * 128, (t + 1) * 128) for t in range(8)] + [(8, NPR, NPR + NS), (9, NPR + NS, NTA)]
    main_src = tiles_of("hT", main_tilecols)

    ebl = A("ebl")
    Sst = A("S").rearrange("p (h v) -> p h v", h=4)
    Sb8 = A("Sb", BF16).rearrange("p (l s v) -> p l s v", l=2, s=4)
    Ktok2 = A("Ktok", BF16).rearrange("p (l c d) -> p l c d", l=2, c=4)
    Ktok = Ktok2[:, 0, :, :]
    accA = [0, 1, 2, 3, 4, 5]
    stash = A("stash")
    stash_b = A("stash", BF16)
    sg = A("sg", BF16).rearrange("p (s t) -> p s t", s=8)
    mixT = A("mixT", BF16).rearrange("p (k t) -> p k t", k=KC)
    qT1 = A("qT1", BF16).rearrange("p (h t) -> p h t", h=4)
    qT2 = A("qT2", BF16).rearrange("p (h t) -> p h t", h=4)
    kTpM = A("kTp", BF16).rearrange("p (h t) -> p h t", h=4)
    VtokM = A("V", BF16).rearrange("p (t c) -> p t c", t=8)

    def mixer_proj(pre):
        sfx = "p" if pre else ""
        ntok = NPRE if pre else NTA
        blocks = BLK_PRE if pre else BLK_A
        nblk = len(blocks)
        srcT = hTpre if pre else hT
        src_tiles = pre_src if pre else main_src
        eT = A("eT" + sfx); lT = A("lT" + sfx); cs = A("cs" + sfx); mq1 = A("mq1" + sfx); mk = A("mk" + sfx)
        mq2 = None if pre else A("mq2")
        nbl = A("nbl" + sfx)
        glrT = A("glrT" + sfx, BF16)
        kTp = A("kTp" + sfx, BF16).rearrange("p (h t) -> p h t", h=4)
        vT = A("vT" + sfx, BF16).rearrange("p (s t) -> p s t", s=2)
        Vtok = A("V" + sfx, BF16).rearrange("p (t c) -> p t c", t=8)
        nchunks = 8
        npc = nchunks * 128
        accA = [0, 1, 2, 3] if pre else [0, 1, 2, 3, 4, 5]
        tbm = 4 if pre else 6
        kbank = 5

        def glr_consumer(bi, b, b0, n):
            P.op("act", lambda e: e.activation(out=glrT[0:16, b0:b0 + n], in_=PSF(b)[0:16, 0:n], func=AF.Copy),
                 reads=psk(b), writes=[("glrT" + sfx, bi)])

        def gate_math(h):
            for bi, (b0, n) in enumerate(blocks):
                b = next_bank(accA)
                P.op("pe", lambda e, b=b, b0=b0, n=n: e.matmul(PSF(b)[:, 0:n], lhsT=wgb[0:16, h * 128:(h + 1) * 128],
                                                               rhs=glrT[0:16, b0:b0 + n], start=True, stop=True),
                     reads=["wgb", ("glrT" + sfx, bi)], writes=psk(b))
                P.op("act", lambda e, b=b, b0=b0, n=n: e.activation(out=eT[:, b0:b0 + n], in_=PSF(b)[:, 0:n], func=AF.Exp,
                                                                    scale=-1.0, bias=vecs[:, h:h + 1]),
                     reads=psk(b) + [("vecs", "nbg")], writes=[("eT" + sfx, bi)])
                P.op("act", lambda e, b0=b0, n=n: e.activation(out=lT[:, b0:b0 + n], in_=eT[:, b0:b0 + n], func=AF.Ln, bias=1.0),
                     reads=[("eT" + sfx, bi)], writes=[("lT" + sfx, bi)])
            allb = [("lT" + sfx, i) for i in range(nblk)]
            P.op("dve", lambda e: e.tensor_tensor_scan(out=cs[:, 0:ntok], data0=resetm[:, 0:ntok], data1=lT[:, 0:ntok], initial=0.0,
                                                       op0=ALU.mult, op1=ALU.add),
                 reads=allb + ["reset"], writes=["cs" + sfx])
            cs_last = cs[:, 0:npc].rearrange("p (c t) -> p c t", t=128)[:, :, 127:128]
            P.op("dve", lambda e: e.tensor_scalar(out=nbl[:, 0:nchunks].rearrange("p (c o) -> p c o", o=1), in0=cs_last,
                                                  scalar1=-1.0 / 16, scalar2=None, op0=ALU.mult),
                 reads=["cs" + sfx], writes=["nbl" + sfx])
            if not pre:
                P.op("act", lambda e: e.activation(out=mq1[:, 0:ntok], in_=cs[:, 0:ntok], func=AF.Exp, scale=-1.0 / 16),
                     reads=["cs" + sfx], writes=["mq1" + sfx])
            P.op("act", lambda e: e.activation(out=ebl[:, h * 8:h * 8 + nchunks], in_=nbl[:, 0:nchunks], func=AF.Exp),
                 reads=["nbl" + sfx], writes=[("ebl", h)])
            for c in range(nchunks):
                P.op("act", lambda e, c=c: e.activation(out=mk[:, c * 128:(c + 1) * 128], in_=cs[:, c * 128:(c + 1) * 128], func=AF.Exp,
                                                        scale=1.0 / 16, bias=nbl[:, c:c + 1]),
                     reads=["cs" + sfx, "nbl" + sfx], writes=[("mk" + sfx, c)])
            mkall = [("mk" + sfx, c) for c in range(nchunks)]
            if not pre:
                P.op("dve", lambda e: e.reciprocal(out=mq2[:, 0:npc], in_=mk[:, 0:npc]), reads=mkall, writes=["mq2"])
                stash_a = stash[:, 64 + h * 16: 64 + (h + 1) * 16]
                P.op("dve", lambda e: e.tensor_copy(out=stash_a, in_=mq1[:, NPR:NPR + NS]), reads=["mq1"], writes=[("stash", "a", h)])
                P.op("dve", lambda e: e.memset(mq1[:, NPR:ntok], 1.0), reads=[("stash", "a", h)], writes=["mq1"])
                P.op("dve", lambda e: e.memset(mk[:, NPR:ntok], 1.0), writes=[("mk", "s")])
                return mkall + [("mk", "s")]
            return mkall

        def v_cons_f(vt):
            slot = vt % 2

            def cons(bi, b, b0, n):
                P.op("act", lambda e: e.activation(out=vT[:, slot, b0:b0 + n], in_=PSF(b)[:, 0:n], func=AF.Copy),
                     reads=psk(b), writes=[("vT" + sfx, slot, bi)])
            return cons

        def v_transposes(vt):
            slot = vt % 2
            tb = tbm
            rk = [("vT" + sfx, slot, bi) for bi in range(nblk)]
            pb = PSB(tb)
            for t in range(8):
                P.op("pe", lambda e, t=t: e.transpose(out=pb[:, t * 128:(t + 1) * 128], in_=vT[:, slot, t * 128:(t + 1) * 128], identity=identb),
                     reads=rk + ["identb"], writes=psk(tb), sig=(t == 7))
            P.op("dve", lambda e: e.tensor_copy(out=Vtok[:, 0:8, vt * 128:(vt + 1) * 128],
                                                in_=pb[:, 0:1024].rearrange("p (t c) -> p t c", t=8)),
                 reads=psk(tb), writes=[("V" + sfx, vt)])
            if not pre:
                Vs = A("Vs", BF16)
                pb2 = PSB(7)
                P.op("pe", lambda e: e.transpose(out=pb2[0:NS, 0:128], in_=vT[:, slot, NPR:NPR + NS], identity=identb),
                     reads=rk + ["identb"], writes=psk(7))
                P.op("act", lambda e: e.activation(out=Vs[0:NS, vt * 128:(vt + 1) * 128], in_=pb2[0:NS, 0:128], func=AF.Copy),
                     reads=psk(7), writes=[("Vs", vt)])

        def prefix_state(h):
            for half in range(2):
                pb = PSB(kbank)
                for ci in range(4):
                    c = half * 4 + ci
                    P.op("pe", lambda e, c=c, ci=ci: e.transpose(out=pb[:, ci * 128:(ci + 1) * 128], in_=kTp[:, h, c * 128:(c + 1) * 128],
                                                                 identity=identb),
                         reads=[("kTp" + sfx, h, c // 4), "identb"], writes=psk(kbank), sig=(ci == 3))
                P.op("act", lambda e: e.activation(out=Ktok2[:, 0, :, :].rearrange("p c d -> p (c d)"), in_=pb[:, 0:512], func=AF.Copy),
                     reads=psk(kbank), writes=[("Ktok", 0)])
                for ci in range(4):
                    c = half * 4 + ci
                    ub = next_bank(accA)
                    P.op("pe", lambda e, c=c, ci=ci, ub=ub: e.matmul(PSF(ub)[:, 0:256], lhsT=Ktok[:, ci, :],
                                                                     rhs=Vtok[:, c, h * 256:(h + 1) * 256], start=True, stop=True),
                         reads=[("Ktok", 0), ("V" + sfx, 2 * h), ("V" + sfx, 2 * h + 1)], writes=psk(ub))
                    P.op("dve", lambda e, c=c, ub=ub: e.scalar_tensor_tensor(out=Sst[:, h, :], in0=Sst[:, h, :],
                                                                            scalar=ebl[:, h * 8 + c: h * 8 + c + 1], in1=PSF(ub)[:, 0:256],
                                                                            op0=ALU.mult, op1=ALU.add),
                         reads=[("S", h), ("ebl", h)] + psk(ub), writes=[("S", h)])

        proj(w_in, C_GLR, 16, srcT, None, src_tiles, ntok, blocks, glr_consumer, accA)
        pending_state = []
        for h in range(4):
            mkkeys = gate_math(h)
            proj2(w_in, C_V + 2 * h * 128, srcT, src_tiles, blocks, [v_cons_f(2 * h), v_cons_f(2 * h + 1)], accA)
            v_transposes(2 * h)
            v_transposes(2 * h + 1)
            if pre and pending_state:
                prefix_state(pending_state.pop(0))

            def k_cons(bi, b, b0, n, h=h, mkkeys=mkkeys):
                P.op("dve", lambda e: e.tensor_tensor(out=kTp[:, h, b0:b0 + n], in0=PSF(b)[:, 0:n], in1=mk[:, b0:b0 + n], op=ALU.mult),
                     reads=psk(b) + mkkeys, writes=[("kTp" + sfx, h, bi)])

            if pre:
                proj(w_in, C_K + h * 128, 128, srcT, None, src_tiles, ntok, blocks, k_cons, accA)
                pending_state.append(h)
                if h == 3:
                    prefix_state(pending_state.pop(0))
                continue

            def q_cons(bi, b, b0, n, h=h):
                P.op("dve", lambda e: e.scalar_tensor_tensor(out=qT1[:, h, b0:b0 + n], in0=PSF(b)[:, 0:n], scalar=128.0 ** -0.5,
                                                             in1=mq1[:, b0:b0 + n], op0=ALU.mult, op1=ALU.mult),
                     reads=psk(b) + ["mq1"], writes=[("qT1", h, bi)])
                n2 = min(b0 + n, NPR) - b0
                if n2 > 0:
                    P.op("dve", lambda e: e.scalar_tensor_tensor(out=qT2[:, h, b0:b0 + n2], in0=PSF(b)[:, 0:n2], scalar=128.0 ** -0.5,
                                                                 in1=mq2[:, b0:b0 + n2], op0=ALU.mult, op1=ALU.mult),
                         reads=psk(b) + ["mq2"], writes=[("qT2", h, bi)])

            def g_cons_f(gt):
                def cons(bi, b, b0, n):
                    n2 = min(b0 + n, NT) - b0
                    P.op("act", lambda e: e.activation(out=sg[:, gt, b0:b0 + n2], in_=PSF(b)[:, 0:n2], func=AF.Silu),
                         reads=psk(b), writes=[("sg", gt, bi)])
                return cons
            proj(w_in, C_Q + h * 128, 128, srcT, None, src_tiles, ntok, blocks, q_cons, accA)
            proj(w_in, C_K + h * 128, 128, srcT, None, src_tiles, ntok, blocks, k_cons, accA)
            P.op("act", lambda e, h=h: e.activation(out=stash_b[:, h * 16:(h + 1) * 16], in_=qT1[:, h, NPR:NPR + NS], func=AF.Copy),
                 reads=[("qT1", h, 2)], writes=[("stash", "q", h)])
            P.op("act", lambda e, h=h: e.activation(out=stash_b[:, 64 + h * 16: 64 + (h + 1) * 16], in_=kTp[:, h, NPR:NPR + NS], func=AF.Copy),
                 reads=[("kTp", h, 2)], writes=[("stash", "k", h)])
            proj2(w_in, C_GOUT + 2 * h * 128, srcT, src_tiles, blocks, [g_cons_f(2 * h), g_cons_f(2 * h + 1)], accA)

    P.phase = 2
    P.op("dve", lambda e: e.memset(A("S"), 0.0), writes=[("S", h) for h in range(4)])
    main_tiles = [(xp[t * 128:(t + 1) * 128, :], 128, t * 128) for t in range(8)] + [(xsm[:, :], NS, NPR)]
    bg_gens.append(norm_transpose(main_tiles, "xsA", "xnA", hT, "hT", NTA, (6, 7), as_gen=True))
    if stage >= 2:
        mixer_proj(True)
    drain()
    tap("S_pre", A("S"), [128, 1024])
    tap("hTpre", A("hTpre", BF16), [128, KC * NPRE])
    tap("csp", A("csp"), [128, NPRE])
    tap("mkp", A("mkp"), [128, NPRE])
    tap("lTp", A("lTp"), [128, NPRE])
    tap("ebl", A("ebl"), [128, 64])
    tap("Vp", A("Vp", BF16), [128, 8192])
    tap("kTpp", A("kTpp", BF16), [128, 4 * NPRE])
    tap("vecs", A("vecs"), [128, 64])

    P.phase = 3
    P.op("pool", lambda e: e.tensor_copy(out=hT3[:, :, NPR + NS:NTA], in_=hTpre3[:, :, NPRE - NH:NPRE]),
         reads=[("hTpre", 7)], writes=[("hT", 9)])

    P.phase = 4
    kTp = kTpM
    Vtok = VtokM
    if stage >= 3:
        mixer_proj(False)

    P.phase = 5
    ATm2 = A("ATm", BF16).rearrange("p (l c i) -> p l c i", l=2, c=4)
    onb = A("on", BF16).rearrange("p (s c) -> p s c", s=2)
    ssg = A("ssg")

    def blk_keys(nm, h, c0, c1, blocks):
        return [(nm, h, bi) for bi, (b0, n) in enumerate(blocks) if b0 < c1 and c0 < b0 + n]

    if stage >= 4:
        def gla_compute(h, half, lane):
                onslot = lane
                BA, BT, BT2 = 4 * lane, 4 * lane + 1, 4 * lane + 1
                ATm = ATm2[:, lane, :, :]
                Ktok = Ktok2[:, lane, :, :]
                Sb4 = Sb8[:, lane, :, :]
                so = 32 * lane
                for ci in range(4):
                    c = half * 4 + ci
                    kk = blk_keys("kTp", h, c * 128, (c + 1) * 128, BLK_A)
                    qk = blk_keys("qT2", h, c * 128, (c + 1) * 128, BLK_A)
                    P.op("pe", lambda e, c=c, ci=ci: e.matmul(PSF(BA)[:, ci * 128:(ci + 1) * 128], lhsT=kTp[:, h, c * 128:(c + 1) * 128],
                                                              rhs=qT2[:, h, c * 128:(c + 1) * 128], start=True, stop=True),
                         reads=kk + qk, writes=psk(BA), sig=(ci == 3))
                for ci in range(4):
                    c = half * 4 + ci
                    kk = blk_keys("kTp", h, c * 128, (c + 1) * 128, BLK_A)
                    P.op("pe", lambda e, c=c, ci=ci: e.transpose(out=PSB(BT)[:, ci * 128:(ci + 1) * 128], in_=kTp[:, h, c * 128:(c + 1) * 128],
                                                                 identity=identb),
                         reads=kk + ["identb"], writes=psk(BT), sig=(ci == 3))
                P.op("dve", lambda e: e.tensor_tensor(
                    out=ATm, in0=PSF(BA)[:, :].rearrange("p (c i) -> p c i", c=4),
                    in1=maskf.rearrange("p (o i) -> p o i", o=1).broadcast_to([128, 4, 128]), op=ALU.mult),
                    reads=psk(BA) + ["mask"], writes=[("ATm", lane)])
                P.op("act", lambda e: e.activation(out=Ktok2[:, lane, :, :].rearrange("p c d -> p (c d)"), in_=PSB(BT)[:, 0:512], func=AF.Copy),
                     reads=psk(BT), writes=[("Ktok", lane)])
                yield
                for ci in range(4):
                    c = half * 4 + ci
                    ub = 4 * lane + 2 + ci // 2
                    uo = (ci % 2) * 256
                    P.op("pe", lambda e, c=c, ci=ci, ub=ub, uo=uo: e.matmul(PSF(ub)[:, uo:uo + 256], lhsT=Ktok[:, ci, :],
                                                                     rhs=Vtok[:, c, h * 256:(h + 1) * 256], start=True, stop=True),
                         reads=[("Ktok", lane), ("V", 2 * h), ("V", 2 * h + 1)], writes=psk(ub))
                for ci in range(4):
                    c = half * 4 + ci
                    ub = 4 * lane + 2 + ci // 2
                    uo = (ci % 2) * 256
                    P.op("dve", lambda e, ci=ci: e.tensor_copy(out=Sb4[:, ci, :], in_=Sst[:, h, :]),
                         reads=[("S", h)], writes=[("Sb", lane, ci)])
                    P.op("dve", lambda e, c=c, ub=ub, uo=uo: e.scalar_tensor_tensor(out=Sst[:, h, :], in0=Sst[:, h, :],
                                                                            scalar=ebl[:, h * 8 + c: h * 8 + c + 1],
                                                                            in1=PSF(ub)[:, uo:uo + 256], op0=ALU.mult, op1=ALU.add),
                         reads=[("S", h), ("ebl", h)] + psk(ub), writes=[("S", h)])
                yield
                for ci in range(4):
                    c = half * 4 + ci
                    ub = 4 * lane + 2 + ci // 2
                    uo = (ci % 2) * 256
                    qk1 = blk_keys("qT1", h, c * 128, (c + 1) * 128, BLK_A)
                    P.op("pe", lambda e, c=c, ci=ci, ub=ub, uo=uo: e.matmul(PSF(ub)[:, uo:uo + 256], lhsT=ATm[:, ci, :],
                                                                     rhs=Vtok[:, c, h * 256:(h + 1) * 256], start=True, stop=False),
                         reads=[("ATm", lane), ("V", 2 * h), ("V", 2 * h + 1)], writes=psk(ub), sig=False)
                    P.op("pe", lambda e, c=c, ub=ub, ci=ci, uo=uo: e.matmul(PSF(ub)[:, uo:uo + 256], lhsT=qT1[:, h, c * 128:(c + 1) * 128],
                                                                     rhs=Sb4[:, ci, :], start=False, stop=True),
                         reads=qk1 + [("Sb", lane, ci)], writes=psk(ub))
                    P.op("act", lambda e, ub=ub, ci=ci, uo=uo: e.activation(out=onb[:, onslot, ci * 256:(ci + 1) * 256], in_=PSF(ub)[:, uo:uo + 256],
                                                                     func=AF.Square, accum_out=ssg[:, so + ci:so + ci + 1]),
                         reads=psk(ub), writes=[("on", onslot, ci), ("ssg", lane, "ss", ci)])
                yield
                P.op("act", lambda e: e.activation(out=ssg[:, so + 8:so + 12], in_=ssg[:, so:so + 4], func=AF.Sqrt, scale=1.0 / 256, bias=1e-6),
                     reads=[("ssg", lane, "ss", ci) for ci in range(4)], writes=[("ssg", lane, "sd")])
                P.op("dve", lambda e: e.reciprocal(out=ssg[:, so + 16:so + 20], in_=ssg[:, so + 8:so + 12]), reads=[("ssg", lane, "sd")], writes=[("ssg", lane, "rs")])
                for ci in range(4):
                    c = half * 4 + ci
                    ub = 4 * lane + 2 + ci // 2
                    uo = (ci % 2) * 256
                    uo = (ci % 2) * 256
                    P.op("act", lambda e, ub=ub, uo=uo, ci=ci: e.activation(out=onb[:, onslot, ci * 256:(ci + 1) * 256], in_=PSF(ub)[:, uo:uo + 256],
                                                                            func=AF.Copy, scale=ssg[:, so + 16 + ci:so + 17 + ci]),
                         reads=psk(ub) + [("ssg", lane, "rs")], writes=[("on", onslot, ci)])
                yield
                for ci in range(4):
                    for dvt in range(2):
                        P.op("pe", lambda e, ci=ci, dvt=dvt: e.transpose(out=PSB(BT2)[:, (ci * 2 + dvt) * 128:(ci * 2 + dvt + 1) * 128],
                                                                         in_=onb[:, onslot, ci * 256 + dvt * 128: ci * 256 + (dvt + 1) * 128],
                                                                         identity=identb),
                             reads=[("on", onslot, ci), "identb"], writes=psk(BT2), sig=(ci == 3 and dvt == 1))
                for dvt in range(2):
                    gt = 2 * h + dvt
                    c0 = half * 512
                    src = PSB(BT2).rearrange("p (c d t) -> p d c t", c=4, d=2)[:, dvt, :, :]
                    gk = blk_keys("sg", gt, c0, c0 + 512, BLK_A)
                    gk = [(k[0], k[1], k[2]) for k in gk]
                    P.op("dve", lambda e, dvt=dvt, gt=gt, c0=c0, src=src: e.scalar_tensor_tensor(
                        out=mixT[:, gt, c0:c0 + 512].rearrange("p (c t) -> p c t", c=4), in0=src, scalar=vecs[:, 4 + dvt:5 + dvt],
                        in1=sg[:, gt, c0:c0 + 512].rearrange("p (c t) -> p c t", c=4), op0=ALU.mult, op1=ALU.mult),
                        reads=psk(BT2) + [("vecs", "gn")] + gk, writes=[("mixT", gt, half)])

        def lane_gen(lane):
            for h in (lane, lane + 2):
                for half in range(2):
                    yield from gla_compute(h, half, lane)
        lanes = [lane_gen(0), lane_gen(1)]
        while lanes:
            for lg in list(lanes):
                try:
                    next(lg)
                except StopIteration:
                    lanes.remove(lg)
        P.dma("sp", "d_glap", glap.rearrange("h k v -> k h v"), Sst, reads=[("S", h) for h in range(4)], final=True)
    tap("mixgla", mixT[:, 0:8, 0:NPR], [128, 8, NPR])
    gn_key = ("vecs", "gn")

    P.phase = 6
    if stage >= 5:
        ktoks = A("ktoks", BF16)
        qexp = A("qexp", BF16).rearrange("p (h s t) -> p h s t", h=4, s=16)
        Vexp = A("Vexp", BF16).rearrange("p (b s v) -> p b s v", b=2, s=16)
        Vs = A("Vs", BF16)
        Ssm = A("Ssm").rearrange("p (b s v) -> p b s v", b=2, s=16)
        Ssb = A("Ssb", BF16).rearrange("p (b v) -> p b v", b=4)
        sgla_r = sgla.rearrange("s h k v -> s h k v")
        glas_r = glas
        for h in range(4):
            P.op("pe", lambda e, h=h: e.transpose(out=PSB(6)[0:NS, h * 128:(h + 1) * 128], in_=stash_b[:, 64 + h * 16: 64 + (h + 1) * 16],
                                                  identity=identb),
                 reads=[("stash", "k", h), "identb"], writes=psk(6), sig=(h == 3))
        P.op("act", lambda e: e.activation(out=ktoks[0:NS, 0:512], in_=PSB(6)[0:NS, 0:512], func=AF.Copy), reads=psk(6), writes=["ktoks"])
        for h in range(4):
            P.op("dve", lambda e, h=h: e.tensor_tensor(
                out=qexp[:, h, :, :], in0=stash_b[:, h * 16:(h + 1) * 16].rearrange("p (o t) -> p o t", o=1).broadcast_to([128, 16, 16]),
                in1=deltar.rearrange("p (s t) -> p s t", s=16), op=ALU.mult),
                reads=[("stash", "q", h), "delta"], writes=[("qexp", h)])

        def vexp_build(h):
            vb = h % 2
            P.op("dve", lambda e: e.tensor_tensor(
                out=Vexp[0:NS, vb, :, :], in0=Vs[0:NS, h * 256:(h + 1) * 256].rearrange("p (o v) -> p o v", o=1).broadcast_to([NS, 16, 256]),
                in1=identf[0:NS, 0:16].rearrange("p (s o) -> p s o", o=1).broadcast_to([NS, 16, 256]), op=ALU.mult),
                reads=[("Vs", 2 * h), ("Vs", 2 * h + 1), "identf"], writes=[("Vexp", vb)])

        def sample_head(h):
            vb = h % 2
            sl = h % 2
            if h == 0:
                vexp_build(0)
            ob = h
            kbanks = [4, 5, 6, 7]

            def psk_mm(s_):
                kb = kbanks[s_ % 4]
                P.op("pe", lambda e, s_=s_, kb=kb: e.matmul(PSF(kb)[:, 0:256], lhsT=ktoks[0:NS, h * 128:(h + 1) * 128],
                                                            rhs=Vexp[0:NS, vb, s_, :], start=True, stop=True),
                     reads=["ktoks", ("Vexp", vb)], writes=psk(kb))
            for s_ in range(3):
                psk_mm(s_)
            if h + 1 < 4:
                vexp_build(h + 1)
            for s_ in range(NS):
                sb = s_ % 4
                kb = kbanks[s_ % 4]
                if s_ + 3 < NS:
                    psk_mm(s_ + 3)
                P.op("dve", lambda e, s_=s_, kb=kb: e.scalar_tensor_tensor(
                    out=Ssm[:, sl, s_, :], in0=Ssm[:, sl, s_, :], scalar=stash[:, 64 + h * 16 + s_: 64 + h * 16 + s_ + 1], in1=PSF(kb)[:, 0:256],
                    op0=ALU.mult, op1=ALU.add),
                    reads=[("Ssm", sl), ("stash", "a", h)] + psk(kb), writes=[("Ssm", sl, s_)])
                P.op("act", lambda e, s_=s_, sb=sb: e.activation(out=Ssb[:, sb, :], in_=Ssm[:, sl, s_, :], func=AF.Copy),
                     reads=[("Ssm", sl, s_)], writes=[("Ssb", sb)])
                P.op("pe", lambda e, s_=s_, sb=sb: e.matmul(PSF(ob)[0:NS, 0:256], lhsT=qexp[:, h, s_, :], rhs=Ssb[:, sb, :],
                                                            start=(s_ == 0), stop=(s_ == NS - 1)),
                     reads=[("qexp", h), ("Ssb", sb)], writes=psk(ob))
            P.dma("sp", f"d_ssm{sl}", glas[:, h, :, :].rearrange("s k v -> k s v"), Ssm[:, sl, :, :],
                  reads=[("Ssm", sl)] + [("Ssm", sl, s_) for s_ in range(NS)], final=True)
            if h + 2 < 4:
                ssm_load(h + 2)

        def ssm_load(h):
            sl = h % 2
            P.dma("sp", f"d_ssm{sl}", Ssm[:, sl, :, :], sgla[:, h, :, :].rearrange("s k v -> k s v"),
                  writes=[("Ssm", sl)] + [("Ssm", sl, s_) for s_ in range(NS)])
        ssm_load(0)
        ssm_load(1)
        for h_ in range(4):
            sample_head(h_)
        for h in range(4):
            P.op("act", lambda e, h=h: e.activation(out=onb[0:NS, 0, h * 256:(h + 1) * 256], in_=PSF(h)[0:NS, 0:256], func=AF.Square,
                                                    accum_out=ssg[0:NS, h:h + 1]),
                 reads=psk(h), writes=[("on", 0, h), ("ssg", "ss", h)])
        P.op("act", lambda e: e.activation(out=ssg[0:NS, 8:12], in_=ssg[0:NS, 0:4], func=AF.Sqrt, scale=1.0 / 256, bias=1e-6),
             reads=[("ssg", "ss", ci) for ci in range(4)], writes=[("ssg", "sd")])
        P.op("dve", lambda e: e.reciprocal(out=ssg[0:NS, 16:20], in_=ssg[0:NS, 8:12]), reads=[("ssg", "sd")], writes=[("ssg", "rs")])
        for h in range(4):
            P.op("act", lambda e, h=h: e.activation(out=onb[0:NS, 0, h * 256:(h + 1) * 256], in_=PSF(h)[0:NS, 0:256], func=AF.Copy,
                                                    scale=ssg[0:NS, 16 + h:17 + h]),
                 reads=psk(h) + [("ssg", "rs")], writes=[("on", 0, h)])
        for vt in range(8):
            P.op("pe", lambda e, vt=vt: e.transpose(out=PSB(7)[:, vt * 16:(vt + 1) * 16], in_=onb[0:NS, 0, vt * 128:(vt + 1) * 128],
                                                    identity=identb[0:NS, 0:NS]),
                 reads=[("on", 0, vt // 2), "identb"], writes=psk(7), sig=(vt == 7))
        for vt in range(8):
            P.op("dve", lambda e, vt=vt: e.scalar_tensor_tensor(out=mixT[:, vt, NPR:NT], in0=PSB(7)[:, vt * 16:(vt + 1) * 16],
                                                                scalar=vecs[:, 4 + vt % 2:5 + vt % 2], in1=sg[:, vt, NPR:NT],
                                                                op0=ALU.mult, op1=ALU.mult),
                 reads=psk(7) + [gn_key, ("sg", vt, 2)], writes=[("mixT", "s", vt)])
    tap("mixs", mixT[:, 0:8, NPR:NT], [128, 8, NS])

    P.phase = 7
    fullg = A("fullg", BF16).rearrange("p (g t) -> p g t", g=8)
    glus = A("glus").rearrange("p (g s) -> p g s", g=8)
    glul = A("glul").rearrange("p (g s) -> p g s", g=8)
    if stage >= 6:
        sgm = A("sgm").rearrange("p (b t) -> p b t", b=2)
        gl32 = A("gl32").rearrange("p (b t) -> p b t", b=2)
        nblk = len(BLK_A)
        wT3 = wT.rearrange("p (g j) -> p g j", g=8)
        sctok = A("sctok")
        scT = A("scT").rearrange("p (g r) -> p g r", g=8)
        ys32 = A("ys32").rearrange("p (a g s) -> p a g s", a=3, g=8)
        otok = A("sctok")
        y32 = A("y32").rearrange("p (g t) -> p g t", g=8)
        ybf = A("ybf", BF16).rearrange("p (b t) -> p b t", b=4)
        ysq = A("ysq", BF16).rearrange("p (b t) -> p b t", b=4)
        lnt = A("lnt").rearrange("p (a t) -> p a t", a=4)
        lnd = A("lnd").rearrange("p (a t) -> p a t", a=2)
        Dm = A("Dm", BF16).rearrange("p (b c) -> p b c", b=16)
        P.dma("sp", "d_cw", sctok[0:31, :], conv_w[:, :], writes=["sctok"])
        for g in range(8):
            P.op("pe", lambda e, g=g: e.transpose(out=PSF(6)[:, g * 32:g * 32 + 31], in_=sctok[0:31, g * 128:(g + 1) * 128],
                                                  identity=identf[0:31, 0:31]),
                 reads=["sctok", "identf"], writes=psk(6), sig=(g == 7))
        P.op("dve", lambda e: e.tensor_copy(out=wT3[:, :, 0:31], in_=PSF(6)[:, 0:256].rearrange("p (g j) -> p g j", g=8)[:, :, 0:31]),
             reads=psk(6), writes=["wT"])


        def ln_block(yget, n, dst_col, s1b, s2b, blk_key):
            P.op("act", lambda e: e.activation(out=lnt[:, 0, 0:n], in_=PSF(s1b)[:, 0:n], func=AF.Copy, scale=1.0 / 1024),
                 reads=psk(s1b), writes=[("lnt", 0)])
            P.op("act", lambda e: e.activation(out=lnt[:, 1, 0:n], in_=PSF(s1b)[:, 0:n], func=AF.Square, scale=1.0 / 1024),
                 reads=psk(s1b), writes=[("lnt", 1)])
            P.op("dve", lambda e: e.scalar_tensor_tensor(out=lnt[:, 2, 0:n], in0=PSF(s2b)[:, 0:n], scalar=1.0 / 1024, in1=lnt[:, 1, 0:n],
                                                         op0=ALU.mult, op1=ALU.subtract),
                 reads=psk(s2b) + [("lnt", 1)], writes=[("lnt", 2)])
            P.op("act", lambda e: e.activation(out=lnt[:, 3, 0:n], in_=lnt[:, 2, 0:n], func=AF.Sqrt, bias=1e-5),
                 reads=[("lnt", 2)], writes=[("lnt", 3)])
            P.op("dve", lambda e: e.reciprocal(out=lnt[:, 1, 0:n], in_=lnt[:, 3, 0:n]), reads=[("lnt", 3)], writes=[("lnt", 1)])
            for g in range(8):
                sl = g % 2
                ysrc, ykeys = yget(g)
                P.op("dve", lambda e, sl=sl, ysrc=ysrc: e.tensor_tensor(out=lnd[:, sl, 0:n], in0=ysrc, in1=lnt[:, 0, 0:n], op=ALU.subtract),
                     reads=ykeys + [("lnt", 0)], writes=[("lnd", sl)])
                P.op("dve", lambda e, sl=sl: e.tensor_tensor(out=lnd[:, sl, 0:n], in0=lnd[:, sl, 0:n], in1=lnt[:, 1, 0:n], op=ALU.mult),
                     reads=[("lnd", sl), ("lnt", 1)], writes=[("lnd", sl)])
                P.op("act", lambda e, sl=sl, g=g: e.activation(out=mixT[:, 8 + g, dst_col:dst_col + n], in_=lnd[:, sl, 0:n], func=AF.Silu,
                                                               scale=vecs[:, 16 + g:17 + g], bias=vecs[:, 24 + g:25 + g]),
                     reads=[("lnd", sl), ("vecs", "lg"), ("vecs", "lb")], writes=[("mixT", 8 + g, blk_key)])


        def sample_round(rt):
            r = [128, 128, 128, 96][rt]
            P.dma("sp", "d_cw", sctok[0:r, :], sconv[rt * 128: rt * 128 + r, :], writes=["sctok"])
            for half in range(2):
                tb = 6 + half
                for gi in range(4):
                    g = half * 4 + gi
                    P.op("pe", lambda e, r=r, g=g, gi=gi, tb=tb: e.transpose(out=PSF(tb)[:, gi * 128: gi * 128 + r],
                                                                             in_=sctok[0:r, g * 128:(g + 1) * 128], identity=identf[0:r, 0:r]),
                         reads=["sctok", "identf"], writes=psk(tb), sig=(gi == 3))
                P.op("act" if half == 0 else "dve",
                     (lambda e, r=r, rt=rt, half=half, tb=tb: e.activation(
                         out=scT[:, half * 4:(half + 1) * 4, rt * 128: rt * 128 + r],
                         in_=PSF(tb)[:, :].rearrange("p (g t) -> p g t", g=4)[:, :, 0:r], func=AF.Copy)) if half == 0 else
                     (lambda e, r=r, rt=rt, half=half, tb=tb: e.tensor_copy(
                         out=scT[:, half * 4:(half + 1) * 4, rt * 128: rt * 128 + r],
                         in_=PSF(tb)[:, :].rearrange("p (g t) -> p g t", g=4)[:, :, 0:r])),
                     reads=psk(tb), writes=[("scT", rt, half)])

        def sample_finish():
            sck = [("scT", rt, half) for rt in range(4) for half in range(2)]
            scT4 = A("scT").rearrange("p (g s j) -> p g s j", g=8, s=16)
            for g in range(8):
                P.op("dve", lambda e, g=g: e.tensor_tensor(out=scT4[:, g, :, :], in0=scT4[:, g, :, :],
                                                           in1=wT3[:, g, 0:30].rearrange("p (o j) -> p o j", o=1).broadcast_to([128, 16, 30]),
                                                           op=ALU.mult),
                     reads=sck + ["wT"], writes=[("scTp", g)])
            P.op("dve", lambda e: e.tensor_reduce(out=ys32[:, 0, :, :].rearrange("p g s -> p (g s)"),
                                                  in_=A("scT").rearrange("p (a j) -> p a j", j=30), axis=AX.X, op=ALU.add),
                 reads=[("scTp", g) for g in range(8)], writes=[("ys32", 0)])
            P.op("dve", lambda e: e.tensor_tensor(out=ys32[:, 1, :, :], in0=glus, in1=wT3[:, :, 30:31].broadcast_to([128, 8, 16]), op=ALU.mult),
                 reads=[("glus", g) for g in range(8)] + ["wT"], writes=[("ys32", 1)])
            P.op("dve", lambda e: e.tensor_tensor(out=ys32[:, 0, :, :], in0=ys32[:, 0, :, :], in1=ys32[:, 1, :, :], op=ALU.add),
                 reads=[("ys32", 0), ("ys32", 1)], writes=[("ys32", 0)])
            P.op("dve", lambda e: e.tensor_tensor(out=ys32[:, 0, :, :], in0=ys32[:, 0, :, :],
                                                  in1=vecs[:, 8:16].rearrange("p (g o) -> p g o", o=1).broadcast_to([128, 8, 16]), op=ALU.add),
                 reads=[("ys32", 0), ("vecs", "cb")], writes=[("ys32", 0)])
            ysb = A("ybf", BF16)[:, 0:128].rearrange("p (g s) -> p g s", g=8)
            ysqb = A("ysq", BF16)[:, 0:128].rearrange("p (g s) -> p g s", g=8)
            P.op("act", lambda e: e.activation(out=ysb, in_=ys32[:, 0, :, :], func=AF.Copy), reads=[("ys32", 0)], writes=[("ybf", 0)])
            P.op("act", lambda e: e.activation(out=ysqb, in_=ys32[:, 0, :, :], func=AF.Square), reads=[("ys32", 0)], writes=[("ysq", 0)])
            for g in range(8):
                P.op("pe", lambda e, g=g: e.matmul(PSF(4)[:, 0:NS], lhsT=onesb, rhs=ysb[:, g, :], start=(g == 0), stop=(g == 7)),
                     reads=["onesb", ("ybf", 0)], writes=psk(4), sig=(g == 7))
            for g in range(8):
                P.op("pe", lambda e, g=g: e.matmul(PSF(5)[:, 0:NS], lhsT=onesb, rhs=ysqb[:, g, :], start=(g == 0), stop=(g == 7)),
                     reads=["onesb", ("ysq", 0)], writes=psk(5), sig=(g == 7))
            ln_block(lambda g: (ys32[:, 0, g, :], [("ys32", 0)]), NS, NPR, 4, 5, "s")


            for half in range(2):
                tb = 6 + half
                for gi in range(4):
                    g = half * 4 + gi
                    P.op("pe", lambda e, g=g, gi=gi, tb=tb: e.transpose(out=PSF(tb)[0:30, gi * 128:(gi + 1) * 128], in_=glul[:, g, 0:30], identity=identf),
                         reads=[("glul", g), "identf"], writes=psk(tb), sig=(gi == 3))
                P.op("act", lambda e, half=half, tb=tb: e.activation(out=otok[0:30, half * 512:(half + 1) * 512], in_=PSF(tb)[0:30, :], func=AF.Copy),
                     reads=psk(tb), writes=["sctok"])
            P.dma("sp", "d_convp", convp[:, :], otok[0:30, :], reads=["sctok"], final=True)

            convs3 = convs.rearrange("(s j) c -> s j c", j=30)
            sconv3 = sconv.rearrange("(s j) c -> s j c", j=30)
            P.dma("sp", "d_convs0", convs3[:, 0:29, :], sconv3[:, 1:30, :], final=True)
            for half in range(2):
                tb = 6 + half
                for gi in range(4):
                    g = half * 4 + gi
                    P.op("pe", lambda e, g=g, gi=gi, tb=tb: e.transpose(out=PSF(tb)[0:NS, gi * 128:(gi + 1) * 128], in_=glus[:, g, :], identity=identf),
                         reads=[("glus", g), "identf"], writes=psk(tb), sig=(gi == 3))
                P.op("act", lambda e, half=half, tb=tb: e.activation(out=otok[0:NS, half * 512:(half + 1) * 512], in_=PSF(tb)[0:NS, :], func=AF.Copy),
                     reads=psk(tb), writes=["sctok"])
            P.dma("sp", "d_convs1", convs3[:, 29, :], otok[0:NS, :], reads=["sctok"], final=True)


        def conv_pair(g0):
            def ug_cons_f(g):
                def ug_cons(bi, b, b0, n):
                    P.op("act", lambda e: e.activation(out=sgm[:, g % 2, b0:b0 + n], in_=PSF(b)[:, 0:n], func=AF.Sigmoid),
                         reads=psk(b), writes=[("sgm", g % 2, bi)])
                return ug_cons

            def ua_cons_f(g):
                gs = g % 2

                def ua_cons(bi, b, b0, n):
                    P.op("dve", lambda e: e.tensor_tensor(out=gl32[:, gs, b0:b0 + n], in0=PSF(b)[:, 0:n], in1=sgm[:, gs, b0:b0 + n], op=ALU.mult),
                         reads=psk(b) + [("sgm", gs, bi)], writes=[("gl32", gs, bi)])
                return ua_cons
            proj2(w_in, C_UG + g0 * 128, hT, main_src, BLK_A, [ug_cons_f(g0), ug_cons_f(g0 + 1)], accA)
            proj2(w_in, C_UA + g0 * 128, hT, main_src, BLK_A, [ua_cons_f(g0), ua_cons_f(g0 + 1)], accA)
            for g in (g0, g0 + 1):
                gs = g % 2
                allk = [("gl32", gs, bi) for bi in range(nblk)]
                P.op("act", lambda e, g=g, gs=gs: e.activation(out=fullg[:, g, 30:30 + NPR], in_=gl32[:, gs, 0:NPR], func=AF.Copy),
                     reads=allk, writes=[("fullg", g, "m")])
                P.op("dve", lambda e, g=g, gs=gs: e.tensor_copy(out=fullg[:, g, 0:30], in_=gl32[:, gs, NTA - 30:NTA]),
                     reads=allk, writes=[("fullg", g, "h")])
                P.op("dve", lambda e, g=g, gs=gs: e.tensor_copy(out=glus[:, g, :], in_=gl32[:, gs, NPR:NT]), reads=allk, writes=[("glus", g)])
                P.op("dve", lambda e, g=g, gs=gs: e.tensor_copy(out=glul[:, g, 0:30], in_=gl32[:, gs, NPR - 30:NPR]), reads=allk, writes=[("glul", g)])
        for gi_, g_ in enumerate(range(0, 8, 2)):
            conv_pair(g_)
            sample_round(gi_)
        sample_finish()

    P.phase = 8
    Wo = [A(n_, BF16).rearrange("p (k c) -> p k c", k=KC) for n_ in ("WoA", "WoB", "WoC", "WoD")]
    Wo_names = ("WoA", "WoB", "WoC", "WoD")
    if stage >= 7:
        for cbk in (0, 1, 2):
            P.dma("pool", f"d_wo{cbk}", Wo[cbk], w_out[:, cbk * 512:(cbk + 1) * 512].rearrange("(k p) c -> p k c", p=128),
                  writes=[Wo_names[cbk]])
        cvt_dmas()
        d_rr = [0]

        def conv_all():
            s1 = [4, 5]
            s2 = [6, 7]

            def conv_mm(g):
                cb = [(g % 2) * 2, (g % 2) * 2 + 1]
                for j in range(31):
                    ds = d_rr[0] % 16
                    d_rr[0] += 1
                    if j % 3 != 2:
                        P.op("dve", lambda e, ds=ds, g=g, j=j: e.tensor_scalar(out=Dm[:, ds, :], in0=identf, scalar1=wT3[:, g, j:j + 1],
                                                                               scalar2=None, op0=ALU.mult),
                             reads=["identf", "wT"], writes=[("Dm", ds)])
                    else:
                        P.op("act", lambda e, ds=ds, g=g, j=j: e.activation(out=Dm[:, ds, :], in_=identf, func=AF.Copy, scale=wT3[:, g, j:j + 1]),
                             reads=["identf", "wT"], writes=[("Dm", ds)])
                    for pb in range(2):
                        P.op("pe", lambda e, ds=ds, g=g, j=j, pb=pb, cbank=cb[pb]: e.matmul(
                            PSF(cbank)[:, 0:512], lhsT=Dm[:, ds, :], rhs=fullg[:, g, pb * 512 + j: pb * 512 + j + 512],
                            start=(j == 0), stop=(j == 30)),
                            reads=[("Dm", ds), ("fullg", g, "m"), ("fullg", g, "h")], writes=psk(cb[pb]))

            def evac_stats(g):
                cb = [(g % 2) * 2, (g % 2) * 2 + 1]
                for pb in range(2):
                    cbank = cb[pb]
                    sl = (g * 2 + pb) % 4
                    P.op("act", lambda e, g=g, cbank=cbank, pb=pb: e.activation(out=y32[:, g, pb * 512:(pb + 1) * 512], in_=PSF(cbank)[:, 0:512],
                                                                                func=AF.Identity, bias=vecs[:, 8 + g:9 + g]),
                         reads=psk(cbank) + [("vecs", "cb")], writes=[("y32", g, pb)])
                    P.op("dve", lambda e, g=g, sl=sl, pb=pb: e.tensor_tensor(out=ysq[:, sl, :], in0=y32[:, g, pb * 512:(pb + 1) * 512],
                                                                             in1=y32[:, g, pb * 512:(pb + 1) * 512], op=ALU.mult),
                         reads=[("y32", g, pb)], writes=[("ysq", sl)])
                    P.op("dve", lambda e, g=g, sl=sl, pb=pb: e.tensor_copy(out=ybf[:, sl, :], in_=y32[:, g, pb * 512:(pb + 1) * 512]),
                         reads=[("y32", g, pb)], writes=[("ybf", sl)])
                    P.op("pe", lambda e, g=g, sl=sl, pb=pb: e.matmul(PSF(s1[pb])[:, 0:512], lhsT=onesb, rhs=ybf[:, sl, :],
                                                                     start=(g == 0), stop=(g == 7)),
                         reads=["onesb", ("ybf", sl)], writes=psk(s1[pb]))
                    P.op("pe", lambda e, g=g, sl=sl, pb=pb: e.matmul(PSF(s2[pb])[:, 0:512], lhsT=onesb, rhs=ysq[:, sl, :],
                                                                     start=(g == 0), stop=(g == 7)),
                         reads=["onesb", ("ysq", sl)], writes=psk(s2[pb]))
            for g in range(8):
                conv_mm(g)
                if g >= 1:
                    evac_stats(g - 1)
            evac_stats(7)
            for pb in range(2):
                ln_block((lambda g, pb=pb: (y32[:, g, pb * 512:(pb + 1) * 512], [("y32", g, pb)])), 512, pb * 512, s1[pb], s2[pb], pb)

        conv_all()
    tap("mixc", mixT[:, 8:16, 0:NT], [128, 8, NT])

    P.phase = 9
    h2T = A("h2T", BF16)
    h2T3 = h2T.rearrange("p (k t) -> p k t", k=KC)
    tok_tiles = [(t, t * 128, 128) for t in range(8)] + [(8, NPR, NS)]
    if stage >= 8:
        for cbk in (3,):
            P.dma("pool", f"d_wo{cbk}", Wo[cbk], w_out[:, cbk * 512:(cbk + 1) * 512].rearrange("(k p) c -> p k c", p=128),
                  writes=[Wo_names[cbk]])
        load_gain(norm_ffn, "d_gbB", "gbB")
        accB = [0, 1, 2, 3, 4, 5]

        def mix_keys(t):
            if t < 8:
                return [("mixT", kc, t // 4) for kc in range(KC)]
            return [("mixT", "s", vt) for vt in range(8)] + [("mixT", 8 + g, "s") for g in range(8)]

        def resid(i, xsl, kx, r):
            t, c0, _ = tok_tiles[i]
            src = xp[c0:c0 + r, :] if t < 8 else xsm[:, :]
            P.dma("sp", f"d_xsB{i % 2}", xsl[0:r, :], src, writes=[kx])
            for cbk in range(4):
                b = next_bank(accB)
                for kc in range(KC):
                    P.op("pe", lambda e, b=b, kc=kc, cbk=cbk: e.matmul(PSF(b)[0:r, 0:512], lhsT=mixT[:, kc, c0:c0 + r], rhs=Wo[cbk][:, kc, :],
                                                                       start=(kc == 0), stop=(kc == KC - 1)),
                         reads=mix_keys(t) + [Wo_names[cbk]], writes=psk(b), sig=(kc == KC - 1))
                P.op("dve", lambda e, b=b, cbk=cbk: e.tensor_tensor(out=xsl[0:r, cbk * 512:(cbk + 1) * 512], in0=PSF(b)[0:r, 0:512],
                                                                    in1=xsl[0:r, cbk * 512:(cbk + 1) * 512], op=ALU.add),
                     reads=psk(b) + [kx], writes=[kx])

        def store_x1(i, xsl, kx, r):
            t, c0, _ = tok_tiles[i]
            P.dma("sp", f"d_x1st{i % 2}", x1d[c0:c0 + r, :], xsl[0:r, :], reads=[kx], writes=[("x1d", i)])

        b_tiles = [(None, r, c0) for (t, c0, r) in tok_tiles]
        norm_transpose(b_tiles, "xsB", "xnB", h2T, "h2T", NT, (6, 7), store_x1=store_x1, resid=resid, gname="gbB")
    tap("h2T", h2T, [128, KC * NT])

    P.phase = 10
    actT = A("actT", BF16).rearrange("p (j t) -> p j t", j=NJ)
    if stage >= 9:
        sgt = A("sgt").rearrange("p (b t) -> p b t", b=6)
        h2_src = tiles_of("h2T", [(t, c0, c0 + r) for (t, c0, r) in tok_tiles])
        accC = [0, 1, 2, 3, 4, 5, 6, 7]

        def ffn_pair(j0):
            def gate_cons_f(j):
                def gate_cons(bi, b, b0, n):
                    P.op("act", lambda e: e.activation(out=sgt[:, (j % 2) * 3 + bi, 0:n], in_=PSF(b)[:, 0:n], func=AF.Silu),
                         reads=psk(b), writes=[("sgt", (j % 2) * 3 + bi)])
                return gate_cons

            def up_cons_f(j):
                def up_cons(bi, b, b0, n):
                    P.op("dve", lambda e: e.tensor_tensor(out=actT[:, j, b0:b0 + n], in0=PSF(b)[:, 0:n], in1=sgt[:, (j % 2) * 3 + bi, 0:n], op=ALU.mult),
                         reads=psk(b) + [("sgt", (j % 2) * 3 + bi)], writes=[("actT", j, bi)])
                return up_cons
            proj2(w_ffn_in, j0 * 128, h2T, h2_src, BLK_C, [gate_cons_f(j0), gate_cons_f(j0 + 1)], accC, wkey="wsC")
            proj2(w_ffn_in, DFF + j0 * 128, h2T, h2_src, BLK_C, [up_cons_f(j0), up_cons_f(j0 + 1)], accC, wkey="wsC")
        for j_ in range(0, NJ, 2):
            ffn_pair(j_)

    P.phase = 11
    if stage >= 10:
        x2 = A("x2").rearrange("p (a c) -> p a c", a=5)
        yst = A("yst").rearrange("p (b t) -> p b t", b=4)
        junk = A("junkD", BF16)
        wsD = A("wsD", BF16).rearrange("p (b j c) -> p b j c", b=4, j=11)
        load_gain(norm_final, "d_gbD", "gbD")
        gbD = A("gbD")
        OBS = [dict(tiles=[0, 1, 2, 3], c0=0, chunks=[(0, 512)]),
               dict(tiles=[4, 5, 6, 7, 8], c0=512, chunks=[(512, 264), (776, 264)])]
        accD = [0, 1, 2, 3, 4, 5]
        wd_rr = [0]

        def do_transposes(ob, m, ysl, nch):
            tl = ob["tiles"]
            ykeys = [("yst", ysl, ci) for ci in range(nch)]
            for li, t in enumerate(tl):
                _, c0, r = tok_tiles[t]
                tb = 6 if li < 4 else 7
                lo = (li % 4) * 128
                off = c0 - ob["c0"]
                last = (li == min(3, len(tl) - 1)) or (li == len(tl) - 1)
                P.op("pe", lambda e, r=r, tb=tb, lo=lo, off=off: e.transpose(out=PSF(tb)[0:r, lo:lo + 128], in_=yst[:, ysl, off:off + r], identity=identf),
                     reads=ykeys + ["identf"], writes=psk(tb), sig=last)
            nfull = min(4, len(tl))
            P.op("dve", lambda e: e.tensor_tensor(out=x2[:, 0:nfull, m * 128:(m + 1) * 128],
                                                  in0=PSF(6)[:, 0:nfull * 128].rearrange("p (a c) -> p a c", a=nfull),
                                                  in1=x2[:, 0:nfull, m * 128:(m + 1) * 128], op=ALU.add),
                 reads=psk(6) + [("x2", li) for li in range(nfull)], writes=[("x2", li) for li in range(nfull)])
            if len(tl) > 4:
                P.op("dve", lambda e: e.tensor_tensor(out=x2[0:NS, 4, m * 128:(m + 1) * 128], in0=PSF(7)[0:NS, 0:128],
                                                      in1=x2[0:NS, 4, m * 128:(m + 1) * 128], op=ALU.add),
                     reads=psk(7) + [("x2", 4)], writes=[("x2", 4)])

        def ffn_out_block(ob):
            tl = ob["tiles"]
            for li, t in enumerate(tl):
                _, c0, r = tok_tiles[t]
                P.dma("sp", f"d_x2l{li}", x2[0:r, li, :], x1d[c0:c0 + r, :], reads=[("x1d", t)], writes=[("x2", li)])
            pending = []
            chunks = ob["chunks"]
            nch = len(chunks)
            for mp in range(8):
                bk = [[next_bank(accD) for ci in range(nch)] for mi in range(2)]
                for kq in range(4):
                    slot = wd_rr[0] % 4
                    wd_rr[0] += 1
                    if mp < NCVT:
                        P.dma("pool", f"d_wsD{slot}", wsD[:, slot, :, :], wbf[mp, kq].rearrange("p (j c) -> p j c", j=11),
                              reads=[("wbf", mp, kq)], writes=[("wsD", slot)])
                    else:
                        P.dma("pool", f"d_wsD{slot}", wsD[:, slot, :, :],
                              w_ffn_out[kq * 1408:(kq + 1) * 1408, mp * 256:(mp + 1) * 256].rearrange("(j p) c -> p j c", p=128),
                              writes=[("wsD", slot)])
                    for mi in range(2):
                        for ci, (c0, n) in enumerate(chunks):
                            b = bk[mi][ci]
                            for jj in range(11):
                                j = kq * 11 + jj
                                P.op("pe", lambda e, b=b, j=j, jj=jj, c0=c0, n=n, slot=slot, mi=mi: e.matmul(
                                    PSF(b)[:, 0:n], lhsT=wsD[:, slot, jj, mi * 128:(mi + 1) * 128], rhs=actT[:, j, c0:c0 + n],
                                    start=(j == 0), stop=(j == NJ - 1)),
                                    reads=[("wsD", slot)] + [("actT", j, bi) for bi in range(3)], writes=psk(b), sig=(jj == 10))
                cur = []
                for mi in range(2):
                    m = mp * 2 + mi
                    ysl = m % 4
                    for ci, (c0, n) in enumerate(chunks):
                        b = bk[mi][ci]
                        off = c0 - ob["c0"]
                        P.op("act", lambda e, b=b, n=n, off=off, ysl=ysl: e.activation(out=yst[:, ysl, off:off + n], in_=PSF(b)[:, 0:n], func=AF.Copy),
                             reads=psk(b), writes=[("yst", ysl, ci)])
                    cur.append((m, ysl))
                for (m, ysl) in pending:
                    do_transposes(ob, m, ysl, nch)
                pending = cur
            for (m, ysl) in pending:
                do_transposes(ob, m, ysl, nch)
            for li, t in enumerate(tl):
                _, c0, r = tok_tiles[t]
                s4 = 8 + 4 * (li % 2)
                P.op("act", lambda e, r=r, li=li, s4=s4: e.activation(out=junk[0:r, :], in_=x2[0:r, li, :], func=AF.Square,
                                                                      accum_out=small[0:r, s4:s4 + 1]),
                     reads=[("x2", li)], writes=["junkD", ("small", "f", li % 2)])
                P.op("act", lambda e, r=r, s4=s4: e.activation(out=small[0:r, s4 + 1:s4 + 2], in_=small[0:r, s4:s4 + 1], func=AF.Sqrt,
                                                               scale=1.0 / D, bias=1e-6),
                     reads=[("small", "f", li % 2)], writes=[("small", "f", li % 2)])
                P.op("dve", lambda e, r=r, s4=s4: e.reciprocal(out=small[0:r, s4 + 2:s4 + 3], in_=small[0:r, s4 + 1:s4 + 2]),
                     reads=[("small", "f", li % 2)], writes=[("small", "f", li % 2)])
                P.op("dve", lambda e, r=r, li=li, s4=s4: e.scalar_tensor_tensor(out=x2[0:r, li, :], in0=x2[0:r, li, :],
                                                                               scalar=small[0:r, s4 + 2:s4 + 3], in1=gbD[0:r, :],
                                                                               op0=ALU.mult, op1=ALU.mult),
                     reads=[("x2", li), ("small", "f", li % 2), "gbD"], writes=[("x2", li)])
                dst = yp[c0:c0 + r, :] if t < 8 else ys[:, :]
                P.dma("sp", f"d_yout{li}", dst, x2[0:r, li, :], reads=[("x2", li)], final=True)
        for ob_ in OBS:
            ffn_out_block(ob_)

    P.wait_tokens("sp", P.final_tokens)
    return nc, P, st, dict(arena=arena, A=A, tap_out=tap_out, top=top, tap=tap)


def _consts():
    ident = np.eye(128, dtype=np.float32)
    mask = np.triu(np.ones((128, 128), dtype=np.float32))
    reset = np.ones((128, NTA), dtype=np.float32)
    reset[:, 0:NPR:128] = 0.0
    reset[:, NPR:] = 0.0
    delta = np.tile(np.eye(16, dtype=np.float32).reshape(1, 256), (128, 1))
    return dict(c_ident=ident, c_mask=mask, c_reset=reset, c_delta=delta)


def make_in_maps(inp):
    f = lambda a: np.ascontiguousarray(np.asarray(a, dtype=np.float32))
    xprompt = f(inp["x_prompt"]); xsample = f(inp["x_sample"])
    sg_ = f(inp["state_gla"])[0]; sc_ = f(inp["state_conv"])[0]
    shared = dict(
        norm_mix=f(inp["norm_mix"])[0], w_in=f(inp["w_in"])[0], w_gate_up=f(inp["w_gate_up"])[0], b_gate=f(inp["b_gate"])[0],
        gla_norm=f(inp["gla_norm"])[0], conv_w=f(inp["conv_w"])[0], conv_b=f(inp["conv_b"])[0], conv_ln_g=f(inp["conv_ln_g"])[0],
        conv_ln_b=f(inp["conv_ln_b"])[0], w_out=f(inp["w_out"])[0], norm_ffn=f(inp["norm_ffn"])[0], w_ffn_in=f(inp["w_ffn_in"])[0],
        w_ffn_out=f(inp["w_ffn_out"])[0], norm_final=f(inp["norm_final"]))
    shared.update(_consts())
    zeros_pre = np.zeros((NPRE, D), dtype=np.float32)
    maps = []
    for c in range(8):
        s, hf = c // 2, c % 2
        m = dict(shared)
        m["xp"] = np.ascontiguousarray(xprompt[s, hf * NPR:(hf + 1) * NPR])
        m["xpre"] = np.ascontiguousarray(xprompt[s, 0:NPRE]) if hf == 1 else zeros_pre
        m["xsm"] = np.ascontiguousarray(xsample[c * NS:(c + 1) * NS, 0])
        m["sgla"] = np.ascontiguousarray(sg_[c * NS:(c + 1) * NS])
        m["sconv"] = np.ascontiguousarray(sc_[c * NS:(c + 1) * NS].reshape(NS * 30, 1024))
        maps.append(m)
    return maps


_CACHE = {}


def kernel(**inputs):
    if "prog" not in _CACHE:
        nc, P, st, info = build_program()
        P.emit()
        _CACHE["prog"] = (nc, P, st)
    nc = _CACHE["prog"][0]
    maps = make_in_maps(inputs)
    res = run_bass_kernel_spmd(nc, maps, core_ids=list(range(8)))
    R = res.results
    y_prompt = np.zeros((4, 2048, D), np.float32)
    y_sample = np.zeros((128, 1, D), np.float32)
    gla_p = np.zeros((1, 4, 4, 128, 256), np.float32)
    conv_p = np.zeros((1, 4, 30, 1024), np.float32)
    gla_s = np.zeros((1, 128, 4, 128, 256), np.float32)
    conv_s = np.zeros((1, 128, 30, 1024), np.float32)
    for c in range(8):
        s, hf = c // 2, c % 2
        y_prompt[s, hf * NPR:(hf + 1) * NPR] = R[c]["yp"]
        y_sample[c * NS:(c + 1) * NS, 0] = R[c]["ys"]
        gla_s[0, c * NS:(c + 1) * NS] = R[c]["glas"]
        conv_s[0, c * NS:(c + 1) * NS] = np.asarray(R[c]["convs"]).reshape(NS, 30, 1024)
        if hf == 1:
            gla_p[0, s] = R[c]["glap"]
            conv_p[0, s] = R[c]["convp"]
    return (y_prompt, y_sample, gla_p, conv_p, gla_s, conv_s)
```

```python
import contextlib
import numpy as np
import concourse.bass as bass
import concourse.mybir as mybir
from concourse.bass_utils import run_bass_kernel_spmd

F32 = mybir.dt.float32
BF16 = mybir.dt.bfloat16
AF = mybir.ActivationFunctionType
ALU = mybir.AluOpType
AX = mybir.AxisListType

D = 2048
KC = 16
NPR = 1024
NS = 16
NH = 32
NTA = NPR + NS + NH
NT = NPR + NS
NPRE = 1024
DFF = 5632
NJ = DFF // 128
C_Q, C_K, C_V, C_GLR, C_GOUT, C_UA, C_UG = 0, 512, 1024, 2048, 2064, 3088, 4112
ARENA_WORDS = 47616


def _split(n0, n, parts):
    out = []
    base = n // parts
    rem = n % parts
    o = n0
    for i in range(parts):
        s = base + (1 if i < rem else 0)
        out.append((o, s))
        o += s
    return out


BLK_A = _split(0, NTA, 3)
BLK_PRE = [(0, 512), (512, 512)]
BLK_C = _split(0, NT, 3)


class Prog:
    ENG = ("pe", "act", "dve", "pool", "sp")

    def __init__(self, nc):
        self.nc = nc
        self.lists = {e: [] for e in self.ENG}
        self.cnt = {e: 0 for e in self.ENG}
        self.waited = {e: {} for e in self.ENG}
        self.lastw = {}
        self.reads = {}
        self.dma_cnt = {}
        self.final_tokens = []
        self.bufs = {}
        self.bufsum = {}
        self.overl = {}
        self.phase = 0
        self.nops = 0

    def plan(self, specs, cap):
        orders = [sorted(specs, key=lambda s: (-(s[3] - s[2]), -s[1])), sorted(specs, key=lambda s: -s[1]),
                  sorted(specs, key=lambda s: (s[2], -s[1])), sorted(specs, key=lambda s: (-s[1] * (s[3] - s[2] + 1)))]
        rng = np.random.default_rng(1234)
        for _ in range(300):
            keys = {s[0]: -s[1] * (s[3] - s[2] + 1) * float(rng.uniform(0.3, 1.0)) for s in specs}
            orders.append(sorted(specs, key=lambda s: keys[s[0]]))
        best = None
        for order in orders:
            placed = []
            ok = True
            for (name, words, f, l) in order:
                words = (words + 7) // 8 * 8
                cands = sorted([(p[1], p[2]) for p in placed if not (p[4] < f or l < p[3])])
                lo = 0
                for (plo, phi) in cands:
                    if lo + words <= plo:
                        break
                    lo = max(lo, phi)
                placed.append((name, lo, lo + words, f, l))
            top = max(p[2] for p in placed)
            if best is None or top < best[0]:
                best = (top, placed)
        top, placed = best
        assert top <= cap, f"arena overflow: {top} > {cap}"
        for (name, lo, hi, f, l) in placed:
            self.bufs[name] = (lo, hi, f, l)
            self.bufsum[name] = {}
        for a in placed:
            self.overl[a[0]] = [b[0] for b in placed if b[0] != a[0] and not (b[2] <= a[1] or a[2] <= b[1])]
        return max(p[2] for p in placed)

    def _bufname(self, k):
        n = k[0] if isinstance(k, tuple) else k
        return n if n in self.bufs else None

    def _deps(self, eng, reads, writes):
        deps = {}

        def add(t):
            if deps.get(t[0], 0) < t[1]:
                deps[t[0]] = t[1]
        touched = set()
        for k in reads:
            if k in self.lastw:
                add(self.lastw[k])
            b = self._bufname(k)
            if b:
                touched.add(b)
        for k in writes:
            if k in self.lastw:
                add(self.lastw[k])
            for t in self.reads.get(k, ()):
                add(t)
            b = self._bufname(k)
            if b:
                touched.add(b)
        for b in touched:
            lo, hi, f, l = self.bufs[b]
            assert f <= self.phase <= l, f"buffer {b} used in phase {self.phase}, declared [{f},{l}]"
            for o in self.overl[b]:
                for s, v in self.bufsum[o].items():
                    add((s, v))
        waits = []
        for s in sorted(deps):
            v = deps[s]
            if eng == "pe" and s == "c_pe":
                continue
            if self.waited[eng].get(s, 0) < v:
                self.waited[eng][s] = v
                waits.append((s, v))
        return waits, touched

    def _commit(self, tok, reads, writes, touched):
        for k in reads:
            self.reads.setdefault(k, set()).add(tok)
        for k in writes:
            self.lastw[k] = tok
            self.reads[k] = set()
        for b in touched:
            d = self.bufsum[b]
            if d.get(tok[0], 0) < tok[1]:
                d[tok[0]] = tok[1]

    def op(self, eng, fn, reads=(), writes=(), sig=True):
        waits, touched = self._deps(eng, reads, writes)
        if sig:
            self.cnt[eng] += 1
            tok = ("c_" + eng, self.cnt[eng])
        else:
            tok = ("c_" + eng, self.cnt[eng] + 1)
        self._commit(tok, reads, writes, touched)
        self.lists[eng].append((waits, fn, ("c_" + eng, 1) if sig else None))
        self.nops += 1
        return tok

    def dma(self, q, sem, out, in_, reads=(), writes=(), final=False, **kw):
        waits, touched = self._deps(q, reads, writes)
        self.dma_cnt[sem] = self.dma_cnt.get(sem, 0) + 1
        tok = (sem, 16 * self.dma_cnt[sem])
        self._commit(tok, reads, writes, touched)
        self.lists[q].append((waits, (lambda e: e.dma_start(out=out, in_=in_, **kw)), (sem, 16)))
        if final:
            self.final_tokens.append(tok)
        self.nops += 1
        return tok

    def wait_tokens(self, eng, toks):
        waits = []
        best = {}
        for (s, v) in toks:
            best[s] = max(best.get(s, 0), v)
        for s in sorted(best):
            v = best[s]
            if self.waited[eng].get(s, 0) < v:
                self.waited[eng][s] = v
                waits.append((s, v))
        if waits:
            self.lists[eng].append((waits, None, None))

    def emit(self):
        nc = self.nc
        semnames = set("c_" + e for e in self.ENG) | set(self.dma_cnt)
        with contextlib.ExitStack() as st:
            sems = {n: st.enter_context(nc.semaphore(n)) for n in sorted(semnames)}
            block = st.enter_context(nc.Block())
            handles = {"pe": block.tensor, "act": block.scalar, "dve": block.vector,
                       "pool": block.gpsimd, "sp": block.sync}

            def make(ename):
                items = self.lists[ename]

                def body(e):
                    for waits, fn, sig in items:
                        for (s, v) in waits:
                            e.wait_ge(sems[s], v)
                        if fn is None:
                            continue
                        ins = fn(e)
                        if sig is not None:
                            ins.then_inc(sems[sig[0]], sig[1])
                return body

            for ename in self.ENG:
                if self.lists[ename]:
                    handles[ename](make(ename))


def build_program(stage=99, taps=()):
    nc = bass.Bass("TRN2", target_bir_lowering=False)
    P = Prog(nc)
    taps = set(taps)

    def din(name, shape):
        return nc.dram_tensor(name, list(shape), F32, kind="ExternalInput").ap()

    def dout(name, shape):
        return nc.dram_tensor(name, list(shape), F32, kind="ExternalOutput").ap()

    xp = din("xp", [NPR, D]); xsm = din("xsm", [NS, D]); xpre = din("xpre", [NPRE, D])
    sgla = din("sgla", [NS, 4, 128, 256]); sconv = din("sconv", [NS * 30, 1024])
    norm_mix = din("norm_mix", [D]); w_in = din("w_in", [D, 5136]); w_gate_up = din("w_gate_up", [16, 512])
    b_gate = din("b_gate", [512]); gla_norm = din("gla_norm", [256]); conv_w = din("conv_w", [31, 1024])
    conv_b = din("conv_b", [1024]); conv_ln_g = din("conv_ln_g", [1024]); conv_ln_b = din("conv_ln_b", [1024])
    w_out = din("w_out", [D, D]); norm_ffn = din("norm_ffn", [D]); w_ffn_in = din("w_ffn_in", [D, 2 * DFF])
    w_ffn_out = din("w_ffn_out", [DFF, D]); norm_final = din("norm_final", [D])
    c_ident = din("c_ident", [128, 128]); c_mask = din("c_mask", [128, 128])
    c_reset = din("c_reset", [128, NTA]); c_delta = din("c_delta", [128, 256])
    yp = dout("yp", [NPR, D]); ys = dout("ys", [NS, D]); glap = dout("glap", [4, 128, 256])
    convp = dout("convp", [30, 1024]); glas = dout("glas", [NS, 4, 128, 256]); convs = dout("convs", [NS * 30, 1024])
    x1d = nc.dram_tensor("x1d", [NT, D], F32, kind="Internal").ap()
    NCVT = 6
    wbf = nc.dram_tensor("wbf", [NCVT, 4, 128, 11 * 256], BF16, kind="Internal").ap()
    tap_out = {}

    LASTP = 11
    specs = [
        ("identf", 128, 0, LASTP), ("identb", 64, 0, LASTP), ("mask", 128, 0, LASTP), ("onesb", 64, 0, LASTP),
        ("reset", NTA, 0, 5), ("delta", 256, 0, 8), ("vecs", 64, 0, LASTP), ("wgb", 256, 0, 5),
        ("wT", 256, 0, 8), ("small", 256, 0, LASTP),
        ("gb", 2048, 0, 3), ("gbB", 2048, 9, 9), ("gbD", 2048, 11, 11),
        ("xsA", 4096, 0, 3), ("xnA", 2048, 0, 3),
        ("xsB", 4096, 9, 9), ("xnB", 2048, 9, 9),
        ("ws", 4096, 2, 7), ("wsC", 4096, 10, 10),
        ("hTpre", 8192, 1, 3), ("hT", KC * NTA // 2, 2, 7),
        ("S", 1024, 2, 5), ("Sb", 8 * 128, 2, 5),
        ("eTp", NPRE, 2, 2), ("lTp", NPRE, 2, 2), ("csp", NPRE, 2, 2), ("mq1p", 8, 2, 2), ("mkp", NPRE, 2, 2),
        ("nblp", 16, 2, 2), ("glrTp", NPRE // 2, 2, 2), ("kTpp", 4 * NPRE // 2, 2, 2), ("vTp", 2 * NPRE // 2, 2, 2), ("Vp", 4096, 2, 2),
        ("eT", NTA, 4, 4), ("lT", NTA, 4, 4), ("cs", NTA, 4, 4), ("mq1", NTA, 4, 4), ("mq2", NTA, 4, 4), ("mk", NTA, 4, 4),
        ("nbl", 16, 4, 4), ("ebl", 64, 2, 5),
        ("glrT", NTA // 2, 4, 4),
        ("qT1", 4 * NTA // 2, 4, 5), ("qT2", 4 * NTA // 2, 4, 5), ("kTp", 4 * NTA // 2, 4, 5),
        ("vT", 2 * NTA // 2, 4, 4), ("V", 4096, 4, 5), ("Vs", 512, 4, 6),
        ("sg", 8 * NT // 2, 4, 6),
        ("mixT", KC * NT // 2, 5, 9),
        ("ATm", 8 * 64, 5, 5), ("Ktok", 8 * 64, 2, 5), ("on", 2 * 512, 5, 6), ("ssg", 64, 5, 6),
        ("stash", 256, 4, 6),
        ("Ssm", 2 * 4096, 6, 6), ("Ssb", 4 * 128, 6, 6), ("Vexp", 2 * 2048, 6, 6), ("qexp", 512, 6, 6), ("ktoks", 256, 6, 6),
        ("sgm", 2 * NTA, 7, 7), ("gl32", 2 * NTA, 7, 7), ("fullg", 8 * 1056 // 2, 7, 8),
        ("glus", 128, 7, 8), ("glul", 256, 7, 8),
        ("Dm", 16 * 64, 8, 8), ("y32", 8192, 8, 8), ("ybf", 4 * 256, 7, 8), ("ysq", 4 * 256, 7, 8),
        ("lnt", 4 * 512, 7, 8), ("lnd", 2 * 512, 7, 8),
        ("sctok", 1024, 7, 7), ("scT", 3840, 7, 7), ("ys32", 3 * 128, 7, 7),
        ("WoA", 4096, 8, 9), ("WoB", 4096, 8, 9), ("WoC", 4096, 8, 9), ("WoD", 4096, 9, 9),
        ("h2T", KC * NT // 2, 9, 10),
        ("actT", NJ * NT // 2, 10, 11), ("sgt", 6 * 352, 10, 10),
        ("wsD", 2 * 2816, 10, 11),
        ("x2", 5 * 2048, 11, 11), ("yst", 4 * 528, 11, 11), ("junkD", 1024, 11, 11),
    ]
    top = P.plan(specs, ARENA_WORDS)

    st = contextlib.ExitStack()
    arena = st.enter_context(nc.sbuf_tensor("arena", [128, ARENA_WORDS], F32))
    banks = [st.enter_context(nc.psum_tensor(f"bank{i}", [128, 512], F32)) for i in range(8)]

    def A(name, dt=F32, off=0, n=None):
        lo, hi, f, l = P.bufs[name]
        a = arena[:, lo + off: (hi if n is None else lo + off + n)]
        return a if dt == F32 else a.bitcast(BF16)

    def PSF(b):
        return banks[b][:, :]

    def PSB(b):
        return banks[b][:, :].bitcast(BF16)

    def tap(name, ap, shape):
        if name not in taps:
            return
        t = dout("tap_" + name, shape)
        tap_out[name] = shape
        q = "pool" if ap.dtype == BF16 else "sp"
        P.wait_tokens(q, list(P.lastw.values()))
        P.dma(q, "d_tap_" + name, t, ap, final=True)

    def tap_bf(name, ap_bf, p, n):
        if name not in taps:
            return
        raise NotImplementedError

    identf = A("identf"); identb = A("identb", BF16); maskf = A("mask"); onesb = A("onesb", BF16)
    resetm = A("reset"); deltar = A("delta"); vecs = A("vecs"); small = A("small"); gb = A("gb")
    wgb = A("wgb", BF16)
    wT = A("wT")
    P.phase = 0
    P.dma("sp", "d_c0", identf, c_ident[:, :], writes=["identf"])
    P.dma("sp", "d_c1", maskf, c_mask[:, :], writes=["mask"])
    P.dma("sp", "d_c2", resetm, c_reset[:, :], writes=["reset"])
    P.dma("sp", "d_c3", deltar, c_delta[:, :], writes=["delta"])
    for (sem_, c0_, c1_, src_, key_) in (("d_c4", 32, 36, b_gate, "bg"), ("d_c5", 4, 6, gla_norm, "gn"), ("d_c6", 8, 16, conv_b, "cb"),
                                       ("d_c7", 16, 24, conv_ln_g, "lg"), ("d_c8", 24, 32, conv_ln_b, "lb")):
        P.dma("sp", sem_, vecs[:, c0_:c1_], src_.rearrange("(h p) -> p h", p=128), writes=[("vecs", key_)],
              allow_slow_non_contiguous=True)
    P.dma("pool", "d_c9", wgb[0:16, 0:512], w_gate_up[:, :], writes=["wgb"])
    P.op("dve", lambda e: e.tensor_copy(out=identb, in_=identf), reads=["identf"], writes=["identb"])
    P.op("dve", lambda e: e.memset(onesb, 1.0), writes=["onesb"])
    P.op("dve", lambda e: e.tensor_scalar(out=vecs[:, 0:4], in0=vecs[:, 32:36], scalar1=-1.0, scalar2=None, op0=ALU.mult),
         reads=[("vecs", "bg")], writes=[("vecs", "nbg")])

    def cvt_dmas():
        for mp in range(NCVT):
            for kq in range(4):
                P.dma("pool", f"d_cvt{mp}_{kq}", wbf[mp, kq].rearrange("p (j c) -> p j c", j=11),
                      w_ffn_out[kq * 1408:(kq + 1) * 1408, mp * 256:(mp + 1) * 256].rearrange("(j p) c -> p j c", p=128),
                      writes=[("wbf", mp, kq)])

    ps_rr = [0]

    def psk(b):
        return [("ps", b, 0), ("ps", b, 1)]

    def psh(b, half):
        return [("ps", b, half)]

    def next_bank(pool):
        b = pool[ps_rr[0] % len(pool)]
        ps_rr[0] += 1
        return b

    ws_rr = [0]

    def load_gain(src, sem, gname="gb"):
        P.dma("sp", sem, A(gname), src.partition_broadcast(128), writes=[gname])

    def norm_transpose(tiles, xs_name, xn_name, dstT, dst_name, ntok_dst, tbanks, store_x1=None, resid=None, gname="gb", as_gen=False):
        xs = [A(xs_name, F32, 0, 2048), A(xs_name, F32, 2048, 2048)]
        xn = [A(xn_name, BF16)[:, 0:2048], A(xn_name, BF16)[:, 2048:4096]]
        dst3 = dstT.rearrange("p (k t) -> p k t", k=KC)
        def part1(i):
            src, r, col = tiles[i]
            s = i % 2
            kx = (xs_name, s); kn = (xn_name, s); kss = ("small", "nss", s)
            if resid is None:
                P.dma("sp", f"d_{xs_name}{s}", xs[s][0:r, :], src, writes=[kx])
            else:
                resid(i, xs[s], kx, r)
            sscol = small[:, 4 * s: 4 * s + 1]
            P.op("act", lambda e, s=s, r=r, sscol=sscol: e.activation(out=xn[s][0:r, :], in_=xs[s][0:r, :], func=AF.Square,
                                                                      accum_out=sscol[0:r, :]),
                 reads=[kx], writes=[kn, kss])
            P.op("act", lambda e, s=s, r=r: e.activation(out=small[0:r, 4 * s + 1: 4 * s + 2], in_=small[0:r, 4 * s: 4 * s + 1],
                                                         func=AF.Sqrt, scale=1.0 / D, bias=1e-6),
                 reads=[kss], writes=[("small", "nsd", s)])
            P.op("dve", lambda e, s=s, r=r: e.reciprocal(out=small[0:r, 4 * s + 2: 4 * s + 3], in_=small[0:r, 4 * s + 1: 4 * s + 2]),
                 reads=[("small", "nsd", s)], writes=[("small", "nrs", s)])
            P.op("dve", lambda e, s=s, r=r: e.scalar_tensor_tensor(out=xn[s][0:r, :], in0=xs[s][0:r, :],
                                                                   scalar=small[0:r, 4 * s + 2: 4 * s + 3], in1=A(gname)[0:r, :],
                                                                   op0=ALU.mult, op1=ALU.mult),
                 reads=[kx, ("small", "nrs", s), gname], writes=[kn])
            if store_x1 is not None:
                store_x1(i, xs[s], kx, r)

        def part2(i):
            src, r, col = tiles[i]
            s = i % 2
            kn = (xn_name, s)
            for half in range(2):
                b = tbanks[half]
                pb = PSB(b)
                for k8 in range(8):
                    kc = half * 8 + k8
                    P.op("pe", lambda e, s=s, r=r, kc=kc, k8=k8, pb=pb: e.transpose(
                        out=pb[:, k8 * 128: k8 * 128 + r], in_=xn[s][0:r, kc * 128:(kc + 1) * 128], identity=identb[0:r, 0:r]),
                        reads=[kn, "identb"], writes=psk(b), sig=(k8 == 7))
                src3 = pb.rearrange("p (k t) -> p k t", k=8)[:, :, 0:r]
                dsto = dst3[:, half * 8:(half + 1) * 8, col:col + r]
                if half == 0:
                    P.op("act", lambda e, src3=src3, dsto=dsto: e.activation(out=dsto, in_=src3, func=AF.Copy),
                         reads=psk(b), writes=[(dst_name, i)])
                else:
                    P.op("dve", lambda e, src3=src3, dsto=dsto: e.tensor_copy(out=dsto, in_=src3),
                         reads=psk(b), writes=[(dst_name, i)])

        n_t = len(tiles)

        def gen():
            part1(0)
            yield
            for i in range(n_t):
                if i + 1 < n_t:
                    part1(i + 1)
                    yield
                part2(i)
                yield
        if as_gen:
            return gen()
        for _ in gen():
            pass

    bg_gens = []

    def tick():
        while bg_gens:
            try:
                next(bg_gens[0])
                return
            except StopIteration:
                bg_gens.pop(0)

    def drain():
        while bg_gens:
            tick()

    def proj(wsrc, c0, M, srcT, src_name, src_tiles, ntok, blocks, consumer, accbanks, wkey="ws"):
        hs = [ws_rr[0] % 8, (ws_rr[0] + 1) % 8]
        ws_rr[0] += 2
        wview = A(wkey, BF16).rearrange("p (s k c) -> p s k c", s=8, k=8)
        wsrc3 = wsrc[:, c0:c0 + M].rearrange("(k p) c -> p k c", p=128)
        for hh in range(2):
            P.dma("pool", f"d_ws{hs[hh]}", wview[:, hs[hh], :, 0:M], wsrc3[:, hh * 8:(hh + 1) * 8, :], writes=[(wkey, hs[hh])])
        src3 = srcT.rearrange("p (k t) -> p k t", k=KC)
        for bi, (b0, n) in enumerate(blocks):
            b = next_bank(accbanks)
            for kc in range(KC):
                P.op("pe", lambda e, b=b, kc=kc, b0=b0, n=n: e.matmul(PSF(b)[0:M, 0:n], lhsT=wview[:, hs[kc // 8], kc % 8, 0:M],
                                                                       rhs=src3[:, kc, b0:b0 + n], start=(kc == 0), stop=(kc == KC - 1)),
                     reads=[(wkey, hs[kc // 8])] + src_tiles(b0, n), writes=psk(b), sig=(kc == KC - 1))
            consumer(bi, b, b0, n)
            tick()

    def proj2(wsrc, c0, srcT, src_tiles, blocks, consumers, accbanks, wkey="ws"):
        qs = [(ws_rr[0] + q) % 8 for q in range(4)]
        ws_rr[0] += 4
        wview = A(wkey, BF16).rearrange("p (s k c) -> p s k c", s=8, k=4)
        for q in range(4):
            P.dma("pool", f"d_ws{qs[q]}", wview[:, qs[q], :, :],
                  wsrc[q * 512:(q + 1) * 512, c0:c0 + 256].rearrange("(k p) c -> p k c", p=128), writes=[(wkey, qs[q])])
        src3 = srcT.rearrange("p (k t) -> p k t", k=KC)
        for mt in range(2):
            for bi, (b0, n) in enumerate(blocks):
                b = next_bank(accbanks)
                for kc in range(KC):
                    P.op("pe", lambda e, b=b, kc=kc, b0=b0, n=n, mt=mt: e.matmul(
                        PSF(b)[:, 0:n], lhsT=wview[:, qs[kc // 4], kc % 4, mt * 128:(mt + 1) * 128],
                        rhs=src3[:, kc, b0:b0 + n], start=(kc == 0), stop=(kc == KC - 1)),
                        reads=[(wkey, qs[kc // 4])] + src_tiles(b0, n), writes=psk(b), sig=(kc == KC - 1))
                consumers[mt](bi, b, b0, n)
                tick()

    def tiles_of(name, tile_cols):
        def f(b0, n):
            return [(name, i) for (i, c0, c1) in tile_cols if c0 < b0 + n and b0 < c1]
        return f

    hTpre = A("hTpre", BF16)
    hT = A("hT", BF16)
    hT3 = hT.rearrange("p (k t) -> p k t", k=KC)
    hTpre3 = hTpre.rearrange("p (k t) -> p k t", k=KC)
    P.phase = 1
    load_gain(norm_mix, "d_gb")
    pre_tiles = [(xpre[t * 128:(t + 1) * 128, :], 128, t * 128) for t in range(8)]
    norm_transpose(pre_tiles, "xsA", "xnA", hTpre, "hTpre", NPRE, (6, 7))
    pre_tilecols = [(t, t * 128, (t + 1) * 128) for t in range(8)]
    pre_src = tiles_of("hTpre", pre_tilecols)
    main_tilecols = [(t, t * 128, (t + 1) * 128) for t in range(8)] + [(8, NPR, NPR + NS), (9, NPR + NS, NTA)]
    main_src = tiles_of("hT", main_tilecols)

    ebl = A("ebl")
    Sst = A("S").rearrange("p (h v) -> p h v", h=4)
    Sb8 = A("Sb", BF16).rearrange("p (l s v) -> p l s v", l=2, s=4)
    Ktok2 = A("Ktok", BF16).rearrange("p (l c d) -> p l c d", l=2, c=4)
    Ktok = Ktok2[:, 0, :, :]
    accA = [0, 1, 2, 3, 4, 5]
    stash = A("stash")
    stash_b = A("stash", BF16)
    sg = A("sg", BF16).rearrange("p (s t) -> p s t", s=8)
    mixT = A("mixT", BF16).rearrange("p (k t) -> p k t", k=KC)
    qT1 = A("qT1", BF16).rearrange("p (h t) -> p h t", h=4)
    qT2 = A("qT2", BF16).rearrange("p (h t) -> p h t", h=4)
    kTpM = A("kTp", BF16).rearrange("p (h t) -> p h t", h=4)
    VtokM = A("V", BF16).rearrange("p (t c) -> p t c", t=8)

    def mixer_proj(pre):
        sfx = "p" if pre else ""
        ntok = NPRE if pre else NTA
        blocks = BLK_PRE if pre else BLK_A
        nblk = len(blocks)
        srcT = hTpre if pre else hT
        src_tiles = pre_src if pre else main_src
        eT = A("eT" + sfx); lT = A("lT" + sfx); cs = A("cs" + sfx); mq1 = A("mq1" + sfx); mk = A("mk" + sfx)
        mq2 = None if pre else A("mq2")
        nbl = A("nbl" + sfx)
        glrT = A("glrT" + sfx, BF16)
        kTp = A("kTp" + sfx, BF16).rearrange("p (h t) -> p h t", h=4)
        vT = A("vT" + sfx, BF16).rearrange("p (s t) -> p s t", s=2)
        Vtok = A("V" + sfx, BF16).rearrange("p (t c) -> p t c", t=8)
        nchunks = 8
        npc = nchunks * 128
        accA = [0, 1, 2, 3] if pre else [0, 1, 2, 3, 4, 5]
        tbm = 4 if pre else 6
        kbank = 5

        def glr_consumer(bi, b, b0, n):
            P.op("act", lambda e: e.activation(out=glrT[0:16, b0:b0 + n], in_=PSF(b)[0:16, 0:n], func=AF.Copy),
                 reads=psk(b), writes=[("glrT" + sfx, bi)])

        def gate_math(h):
            for bi, (b0, n) in enumerate(blocks):
                b = next_bank(accA)
                P.op("pe", lambda e, b=b, b0=b0, n=n: e.matmul(PSF(b)[:, 0:n], lhsT=wgb[0:16, h * 128:(h + 1) * 128],
                                                               rhs=glrT[0:16, b0:b0 + n], start=True, stop=True),
                     reads=["wgb", ("glrT" + sfx, bi)], writes=psk(b))
                P.op("act", lambda e, b=b, b0=b0, n=n: e.activation(out=eT[:, b0:b0 + n], in_=PSF(b)[:, 0:n], func=AF.Exp,
                                                                    scale=-1.0, bias=vecs[:, h:h + 1]),
                     reads=psk(b) + [("vecs", "nbg")], writes=[("eT" + sfx, bi)])
                P.op("act", lambda e, b0=b0, n=n: e.activation(out=lT[:, b0:b0 + n], in_=eT[:, b0:b0 + n], func=AF.Ln, bias=1.0),
                     reads=[("eT" + sfx, bi)], writes=[("lT" + sfx, bi)])
            allb = [("lT" + sfx, i) for i in range(nblk)]
            P.op("dve", lambda e: e.tensor_tensor_scan(out=cs[:, 0:ntok], data0=resetm[:, 0:ntok], data1=lT[:, 0:ntok], initial=0.0,
                                                       op0=ALU.mult, op1=ALU.add),
                 reads=allb + ["reset"], writes=["cs" + sfx])
            cs_last = cs[:, 0:npc].rearrange("p (c t) -> p c t", t=128)[:, :, 127:128]
            P.op("dve", lambda e: e.tensor_scalar(out=nbl[:, 0:nchunks].rearrange("p (c o) -> p c o", o=1), in0=cs_last,
                                                  scalar1=-1.0 / 16, scalar2=None, op0=ALU.mult),
                 reads=["cs" + sfx], writes=["nbl" + sfx])
            if not pre:
                P.op("act", lambda e: e.activation(out=mq1[:, 0:ntok], in_=cs[:, 0:ntok], func=AF.Exp, scale=-1.0 / 16),
                     reads=["cs" + sfx], writes=["mq1" + sfx])
            P.op("act", lambda e: e.activation(out=ebl[:, h * 8:h * 8 + nchunks], in_=nbl[:, 0:nchunks], func=AF.Exp),
                 reads=["nbl" + sfx], writes=[("ebl", h)])
            for c in range(nchunks):
                P.op("act", lambda e, c=c: e.activation(out=mk[:, c * 128:(c + 1) * 128], in_=cs[:, c * 128:(c + 1) * 128], func=AF.Exp,
                                                        scale=1.0 / 16, bias=nbl[:, c:c + 1]),
                     reads=["cs" + sfx, "nbl" + sfx], writes=[("mk" + sfx, c)])
            mkall = [("mk" + sfx, c) for c in range(nchunks)]
            if not pre:
                P.op("dve", lambda e: e.reciprocal(out=mq2[:, 0:npc], in_=mk[:, 0:npc]), reads=mkall, writes=["mq2"])
                stash_a = stash[:, 64 + h * 16: 64 + (h + 1) * 16]
                P.op("dve", lambda e: e.tensor_copy(out=stash_a, in_=mq1[:, NPR:NPR + NS]), reads=["mq1"], writes=[("stash", "a", h)])
                P.op("dve", lambda e: e.memset(mq1[:, NPR:ntok], 1.0), reads=[("stash", "a", h)], writes=["mq1"])
                P.op("dve", lambda e: e.memset(mk[:, NPR:ntok], 1.0), writes=[("mk", "s")])
                return mkall + [("mk", "s")]
            return mkall

        def v_cons_f(vt):
            slot = vt % 2

            def cons(bi, b, b0, n):
                P.op("act", lambda e: e.activation(out=vT[:, slot, b0:b0 + n], in_=PSF(b)[:, 0:n], func=AF.Copy),
                     reads=psk(b), writes=[("vT" + sfx, slot, bi)])
            return cons

        def v_transposes(vt):
            slot = vt % 2
            tb = tbm
            rk = [("vT" + sfx, slot, bi) for bi in range(nblk)]
            pb = PSB(tb)
            for t in range(8):
                P.op("pe", lambda e, t=t: e.transpose(out=pb[:, t * 128:(t + 1) * 128], in_=vT[:, slot, t * 128:(t + 1) * 128], identity=identb),
                     reads=rk + ["identb"], writes=psk(tb), sig=(t == 7))
            P.op("dve", lambda e: e.tensor_copy(out=Vtok[:, 0:8, vt * 128:(vt + 1) * 128],
                                                in_=pb[:, 0:1024].rearrange("p (t c) -> p t c", t=8)),
                 reads=psk(tb), writes=[("V" + sfx, vt)])
            if not pre:
                Vs = A("Vs", BF16)
                pb2 = PSB(7)
                P.op("pe", lambda e: e.transpose(out=pb2[0:NS, 0:128], in_=vT[:, slot, NPR:NPR + NS], identity=identb),
                     reads=rk + ["identb"], writes=psk(7))
                P.op("act", lambda e: e.activation(out=Vs[0:NS, vt * 128:(vt + 1) * 128], in_=pb2[0:NS, 0:128], func=AF.Copy),
                     reads=psk(7), writes=[("Vs", vt)])

        def prefix_state(h):
            for half in range(2):
                pb = PSB(kbank)
                for ci in range(4):
                    c = half * 4 + ci
                    P.op("pe", lambda e, c=c, ci=ci: e.transpose(out=pb[:, ci * 128:(ci + 1) * 128], in_=kTp[:, h, c * 128:(c + 1) * 128],
                                                                 identity=identb),
                         reads=[("kTp" + sfx, h, c // 4), "identb"], writes=psk(kbank), sig=(ci == 3))
                P.op("act", lambda e: e.activation(out=Ktok2[:, 0, :, :].rearrange("p c d -> p (c d)"), in_=pb[:, 0:512], func=AF.Copy),
                     reads=psk(kbank), writes=[("Ktok", 0)])
                for ci in range(4):
                    c = half * 4 + ci
                    ub = next_bank(accA)
                    P.op("pe", lambda e, c=c, ci=ci, ub=ub: e.matmul(PSF(ub)[:, 0:256], lhsT=Ktok[:, ci, :],
                                                                     rhs=Vtok[:, c, h * 256:(h + 1) * 256], start=True, stop=True),
                         reads=[("Ktok", 0), ("V" + sfx, 2 * h), ("V" + sfx, 2 * h + 1)], writes=psk(ub))
                    P.op("dve", lambda e, c=c, ub=ub: e.scalar_tensor_tensor(out=Sst[:, h, :], in0=Sst[:, h, :],
                                                                            scalar=ebl[:, h * 8 + c: h * 8 + c + 1], in1=PSF(ub)[:, 0:256],
                                                                            op0=ALU.mult, op1=ALU.add),
                         reads=[("S", h), ("ebl", h)] + psk(ub), writes=[("S", h)])

        proj(w_in, C_GLR, 16, srcT, None, src_tiles, ntok, blocks, glr_consumer, accA)
        pending_state = []
        for h in range(4):
            mkkeys = gate_math(h)
            proj2(w_in, C_V + 2 * h * 128, srcT, src_tiles, blocks, [v_cons_f(2 * h), v_cons_f(2 * h + 1)], accA)
            v_transposes(2 * h)
            v_transposes(2 * h + 1)
            if pre and pending_state:
                prefix_state(pending_state.pop(0))

            def k_cons(bi, b, b0, n, h=h, mkkeys=mkkeys):
                P.op("dve", lambda e: e.tensor_tensor(out=kTp[:, h, b0:b0 + n], in0=PSF(b)[:, 0:n], in1=mk[:, b0:b0 + n], op=ALU.mult),
                     reads=psk(b) + mkkeys, writes=[("kTp" + sfx, h, bi)])

            if pre:
                proj(w_in, C_K + h * 128, 128, srcT, None, src_tiles, ntok, blocks, k_cons, accA)
                pending_state.append(h)
                if h == 3:
                    prefix_state(pending_state.pop(0))
                continue

            def q_cons(bi, b, b0, n, h=h):
                P.op("dve", lambda e: e.scalar_tensor_tensor(out=qT1[:, h, b0:b0 + n], in0=PSF(b)[:, 0:n], scalar=128.0 ** -0.5,
                                                             in1=mq1[:, b0:b0 + n], op0=ALU.mult, op1=ALU.mult),
                     reads=psk(b) + ["mq1"], writes=[("qT1", h, bi)])
                n2 = min(b0 + n, NPR) - b0
                if n2 > 0:
                    P.op("dve", lambda e: e.scalar_tensor_tensor(out=qT2[:, h, b0:b0 + n2], in0=PSF(b)[:, 0:n2], scalar=128.0 ** -0.5,
                                                                 in1=mq2[:, b0:b0 + n2], op0=ALU.mult, op1=ALU.mult),
                         reads=psk(b) + ["mq2"], writes=[("qT2", h, bi)])

            def g_cons_f(gt):
                def cons(bi, b, b0, n):
                    n2 = min(b0 + n, NT) - b0
                    P.op("act", lambda e: e.activation(out=sg[:, gt, b0:b0 + n2], in_=PSF(b)[:, 0:n2], func=AF.Silu),
                         reads=psk(b), writes=[("sg", gt, bi)])
                return cons
            proj(w_in, C_Q + h * 128, 128, srcT, None, src_tiles, ntok, blocks, q_cons, accA)
            proj(w_in, C_K + h * 128, 128, srcT, None, src_tiles, ntok, blocks, k_cons, accA)
            P.op("act", lambda e, h=h: e.activation(out=stash_b[:, h * 16:(h + 1) * 16], in_=qT1[:, h, NPR:NPR + NS], func=AF.Copy),
                 reads=[("qT1", h, 2)], writes=[("stash", "q", h)])
            P.op("act", lambda e, h=h: e.activation(out=stash_b[:, 64 + h * 16: 64 + (h + 1) * 16], in_=kTp[:, h, NPR:NPR + NS], func=AF.Copy),
                 reads=[("kTp", h, 2)], writes=[("stash", "k", h)])
            proj2(w_in, C_GOUT + 2 * h * 128, srcT, src_tiles, blocks, [g_cons_f(2 * h), g_cons_f(2 * h + 1)], accA)

    P.phase = 2
    P.op("dve", lambda e: e.memset(A("S"), 0.0), writes=[("S", h) for h in range(4)])
    main_tiles = [(xp[t * 128:(t + 1) * 128, :], 128, t * 128) for t in range(8)] + [(xsm[:, :], NS, NPR)]
    bg_gens.append(norm_transpose(main_tiles, "xsA", "xnA", hT, "hT", NTA, (6, 7), as_gen=True))
    if stage >= 2:
        mixer_proj(True)
    drain()
    tap("S_pre", A("S"), [128, 1024])
    tap("hTpre", A("hTpre", BF16), [128, KC * NPRE])
    tap("csp", A("csp"), [128, NPRE])
    tap("mkp", A("mkp"), [128, NPRE])
    tap("lTp", A("lTp"), [128, NPRE])
    tap("ebl", A("ebl"), [128, 64])
    tap("Vp", A("Vp", BF16), [128, 8192])
    tap("kTpp", A("kTpp", BF16), [128, 4 * NPRE])
    tap("vecs", A("vecs"), [128, 64])

    P.phase = 3
    P.op("pool", lambda e: e.tensor_copy(out=hT3[:, :, NPR + NS:NTA], in_=hTpre3[:, :, NPRE - NH:NPRE]),
         reads=[("hTpre", 7)], writes=[("hT", 9)])

    P.phase = 4
    kTp = kTpM
    Vtok = VtokM
    if stage >= 3:
        mixer_proj(False)

    P.phase = 5
    ATm2 = A("ATm", BF16).rearrange("p (l c i) -> p l c i", l=2, c=4)
    onb = A("on", BF16).rearrange("p (s c) -> p s c", s=2)
    ssg = A("ssg")

    def blk_keys(nm, h, c0, c1, blocks):
        return [(nm, h, bi) for bi, (b0, n) in enumerate(blocks) if b0 < c1 and c0 < b0 + n]

    if stage >= 4:
        def gla_compute(h, half, lane):
                onslot = lane
                BA, BT, BT2 = 4 * lane, 4 * lane + 1, 4 * lane + 1
                ATm = ATm2[:, lane, :, :]
                Ktok = Ktok2[:, lane, :, :]
                Sb4 = Sb8[:, lane, :, :]
                so = 32 * lane
                for ci in range(4):
                    c = half * 4 + ci
                    kk = blk_keys("kTp", h, c * 128, (c + 1) * 128, BLK_A)
                    qk = blk_keys("qT2", h, c * 128, (c + 1) * 128, BLK_A)
                    P.op("pe", lambda e, c=c, ci=ci: e.matmul(PSF(BA)[:, ci * 128:(ci + 1) * 128], lhsT=kTp[:, h, c * 128:(c + 1) * 128],
                                                              rhs=qT2[:, h, c * 128:(c + 1) * 128], start=True, stop=True),
                         reads=kk + qk, writes=psk(BA), sig=(ci == 3))
                for ci in range(4):
                    c = half * 4 + ci
                    kk = blk_keys("kTp", h, c * 128, (c + 1) * 128, BLK_A)
                    P.op("pe", lambda e, c=c, ci=ci: e.transpose(out=PSB(BT)[:, ci * 128:(ci + 1) * 128], in_=kTp[:, h, c * 128:(c + 1) * 128],
                                                                 identity=identb),
                         reads=kk + ["identb"], writes=psk(BT), sig=(ci == 3))
                P.op("dve", lambda e: e.tensor_tensor(
                    out=ATm, in0=PSF(BA)[:, :].rearrange("p (c i) -> p c i", c=4),
                    in1=maskf.rearrange("p (o i) -> p o i", o=1).broadcast_to([128, 4, 128]), op=ALU.mult),
                    reads=psk(BA) + ["mask"], writes=[("ATm", lane)])
                P.op("act", lambda e: e.activation(out=Ktok2[:, lane, :, :].rearrange("p c d -> p (c d)"), in_=PSB(BT)[:, 0:512], func=AF.Copy),
                     reads=psk(BT), writes=[("Ktok", lane)])
                yield
                for ci in range(4):
                    c = half * 4 + ci
                    ub = 4 * lane + 2 + ci // 2
                    uo = (ci % 2) * 256
                    P.op("pe", lambda e, c=c, ci=ci, ub=ub, uo=uo: e.matmul(PSF(ub)[:, uo:uo + 256], lhsT=Ktok[:, ci, :],
                                                                     rhs=Vtok[:, c, h * 256:(h + 1) * 256], start=True, stop=True),
                         reads=[("Ktok", lane), ("V", 2 * h), ("V", 2 * h + 1)], writes=psk(ub))
                for ci in range(4):
                    c = half * 4 + ci
                    ub = 4 * lane + 2 + ci // 2
                    uo = (ci % 2) * 256
                    P.op("dve", lambda e, ci=ci: e.tensor_copy(out=Sb4[:, ci, :], in_=Sst[:, h, :]),
                         reads=[("S", h)], writes=[("Sb", lane, ci)])
                    P.op("dve", lambda e, c=c, ub=ub, uo=uo: e.scalar_tensor_tensor(out=Sst[:, h, :], in0=Sst[:, h, :],
                                                                            scalar=ebl[:, h * 8 + c: h * 8 + c + 1],
                                                                            in1=PSF(ub)[:, uo:uo + 256], op0=ALU.mult, op1=ALU.add),
                         reads=[("S", h), ("ebl", h)] + psk(ub), writes=[("S", h)])
                yield
                for ci in range(4):
                    c = half * 4 + ci
                    ub = 4 * lane + 2 + ci // 2
                    uo = (ci % 2) * 256
                    qk1 = blk_keys("qT1", h, c * 128, (c + 1) * 128, BLK_A)
                    P.op("pe", lambda e, c=c, ci=ci, ub=ub, uo=uo: e.matmul(PSF(ub)[:, uo:uo + 256], lhsT=ATm[:, ci, :],
                                                                     rhs=Vtok[:, c, h * 256:(h + 1) * 256], start=True, stop=False),
                         reads=[("ATm", lane), ("V", 2 * h), ("V", 2 * h + 1)], writes=psk(ub), sig=False)
                    P.op("pe", lambda e, c=c, ub=ub, ci=ci, uo=uo: e.matmul(PSF(ub)[:, uo:uo + 256], lhsT=qT1[:, h, c * 128:(c + 1) * 128],
                                                                     rhs=Sb4[:, ci, :], start=False, stop=True),
                         reads=qk1 + [("Sb", lane, ci)], writes=psk(ub))
                    P.op("act", lambda e, ub=ub, ci=ci, uo=uo: e.activation(out=onb[:, onslot, ci * 256:(ci + 1) * 256], in_=PSF(ub)[:, uo:uo + 256],
                                                                     func=AF.Square, accum_out=ssg[:, so + ci:so + ci + 1]),
                         reads=psk(ub), writes=[("on", onslot, ci), ("ssg", lane, "ss", ci)])
                yield
                P.op("act", lambda e: e.activation(out=ssg[:, so + 8:so + 12], in_=ssg[:, so:so + 4], func=AF.Sqrt, scale=1.0 / 256, bias=1e-6),
                     reads=[("ssg", lane, "ss", ci) for ci in range(4)], writes=[("ssg", lane, "sd")])
                P.op("dve", lambda e: e.reciprocal(out=ssg[:, so + 16:so + 20], in_=ssg[:, so + 8:so + 12]), reads=[("ssg", lane, "sd")], writes=[("ssg", lane, "rs")])
                for ci in range(4):
                    c = half * 4 + ci
                    ub = 4 * lane + 2 + ci // 2
                    uo = (ci % 2) * 256
                    uo = (ci % 2) * 256
                    P.op("act", lambda e, ub=ub, uo=uo, ci=ci: e.activation(out=onb[:, onslot, ci * 256:(ci + 1) * 256], in_=PSF(ub)[:, uo:uo + 256],
                                                                            func=AF.Copy, scale=ssg[:, so + 16 + ci:so + 17 + ci]),
                         reads=psk(ub) + [("ssg", lane, "rs")], writes=[("on", onslot, ci)])
                yield
                for ci in range(4):
                    for dvt in range(2):
                        P.op("pe", lambda e, ci=ci, dvt=dvt: e.transpose(out=PSB(BT2)[:, (ci * 2 + dvt) * 128:(ci * 2 + dvt + 1) * 128],
                                                                         in_=onb[:, onslot, ci * 256 + dvt * 128: ci * 256 + (dvt + 1) * 128],
                                                                         identity=identb),
                             reads=[("on", onslot, ci), "identb"], writes=psk(BT2), sig=(ci == 3 and dvt == 1))
                for dvt in range(2):
                    gt = 2 * h + dvt
                    c0 = half * 512
                    src = PSB(BT2).rearrange("p (c d t) -> p d c t", c=4, d=2)[:, dvt, :, :]
                    gk = blk_keys("sg", gt, c0, c0 + 512, BLK_A)
                    gk = [(k[0], k[1], k[2]) for k in gk]
                    P.op("dve", lambda e, dvt=dvt, gt=gt, c0=c0, src=src: e.scalar_tensor_tensor(
                        out=mixT[:, gt, c0:c0 + 512].rearrange("p (c t) -> p c t", c=4), in0=src, scalar=vecs[:, 4 + dvt:5 + dvt],
                        in1=sg[:, gt, c0:c0 + 512].rearrange("p (c t) -> p c t", c=4), op0=ALU.mult, op1=ALU.mult),
                        reads=psk(BT2) + [("vecs", "gn")] + gk, writes=[("mixT", gt, half)])

        def lane_gen(lane):
            for h in (lane, lane + 2):
                for half in range(2):
                    yield from gla_compute(h, half, lane)
        lanes = [lane_gen(0), lane_gen(1)]
        while lanes:
            for lg in list(lanes):
                try:
                    next(lg)
                except StopIteration:
                    lanes.remove(lg)
        P.dma("sp", "d_glap", glap.rearrange("h k v -> k h v"), Sst, reads=[("S", h) for h in range(4)], final=True)
    tap("mixgla", mixT[:, 0:8, 0:NPR], [128, 8, NPR])
    gn_key = ("vecs", "gn")

    P.phase = 6
    if stage >= 5:
        ktoks = A("ktoks", BF16)
        qexp = A("qexp", BF16).rearrange("p (h s t) -> p h s t", h=4, s=16)
        Vexp = A("Vexp", BF16).rearrange("p (b s v) -> p b s v", b=2, s=16)
        Vs = A("Vs", BF16)
        Ssm = A("Ssm").rearrange("p (b s v) -> p b s v", b=2, s=16)
        Ssb = A("Ssb", BF16).rearrange("p (b v) -> p b v", b=4)
        sgla_r = sgla.rearrange("s h k v -> s h k v")
        glas_r = glas
        for h in range(4):
            P.op("pe", lambda e, h=h: e.transpose(out=PSB(6)[0:NS, h * 128:(h + 1) * 128], in_=stash_b[:, 64 + h * 16: 64 + (h + 1) * 16],
                                                  identity=identb),
                 reads=[("stash", "k", h), "identb"], writes=psk(6), sig=(h == 3))
        P.op("act", lambda e: e.activation(out=ktoks[0:NS, 0:512], in_=PSB(6)[0:NS, 0:512], func=AF.Copy), reads=psk(6), writes=["ktoks"])
        for h in range(4):
            P.op("dve", lambda e, h=h: e.tensor_tensor(
                out=qexp[:, h, :, :], in0=stash_b[:, h * 16:(h + 1) * 16].rearrange("p (o t) -> p o t", o=1).broadcast_to([128, 16, 16]),
                in1=deltar.rearrange("p (s t) -> p s t", s=16), op=ALU.mult),
                reads=[("stash", "q", h), "delta"], writes=[("qexp", h)])

        def vexp_build(h):
            vb = h % 2
            P.op("dve", lambda e: e.tensor_tensor(
                out=Vexp[0:NS, vb, :, :], in0=Vs[0:NS, h * 256:(h + 1) * 256].rearrange("p (o v) -> p o v", o=1).broadcast_to([NS, 16, 256]),
                in1=identf[0:NS, 0:16].rearrange("p (s o) -> p s o", o=1).broadcast_to([NS, 16, 256]), op=ALU.mult),
                reads=[("Vs", 2 * h), ("Vs", 2 * h + 1), "identf"], writes=[("Vexp", vb)])

        def sample_head(h):
            vb = h % 2
            sl = h % 2
            if h == 0:
                vexp_build(0)
            ob = h
            kbanks = [4, 5, 6, 7]

            def psk_mm(s_):
                kb = kbanks[s_ % 4]
                P.op("pe", lambda e, s_=s_, kb=kb: e.matmul(PSF(kb)[:, 0:256], lhsT=ktoks[0:NS, h * 128:(h + 1) * 128],
                                                            rhs=Vexp[0:NS, vb, s_, :], start=True, stop=True),
                     reads=["ktoks", ("Vexp", vb)], writes=psk(kb))
            for s_ in range(3):
                psk_mm(s_)
            if h + 1 < 4:
                vexp_build(h + 1)
            for s_ in range(NS):
                sb = s_ % 4
                kb = kbanks[s_ % 4]
                if s_ + 3 < NS:
                    psk_mm(s_ + 3)
                P.op("dve", lambda e, s_=s_, kb=kb: e.scalar_tensor_tensor(
                    out=Ssm[:, sl, s_, :], in0=Ssm[:, sl, s_, :], scalar=stash[:, 64 + h * 16 + s_: 64 + h * 16 + s_ + 1], in1=PSF(kb)[:, 0:256],
                    op0=ALU.mult, op1=ALU.add),
                    reads=[("Ssm", sl), ("stash", "a", h)] + psk(kb), writes=[("Ssm", sl, s_)])
                P.op("act", lambda e, s_=s_, sb=sb: e.activation(out=Ssb[:, sb, :], in_=Ssm[:, sl, s_, :], func=AF.Copy),
                     reads=[("Ssm", sl, s_)], writes=[("Ssb", sb)])
                P.op("pe", lambda e, s_=s_, sb=sb: e.matmul(PSF(ob)[0:NS, 0:256], lhsT=qexp[:, h, s_, :], rhs=Ssb[:, sb, :],
                                                            start=(s_ == 0), stop=(s_ == NS - 1)),
                     reads=[("qexp", h), ("Ssb", sb)], writes=psk(ob))
            P.dma("sp", f"d_ssm{sl}", glas[:, h, :, :].rearrange("s k v -> k s v"), Ssm[:, sl, :, :],
                  reads=[("Ssm", sl)] + [("Ssm", sl, s_) for s_ in range(NS)], final=True)
            if h + 2 < 4:
                ssm_load(h + 2)

        def ssm_load(h):
            sl = h % 2
            P.dma("sp", f"d_ssm{sl}", Ssm[:, sl, :, :], sgla[:, h, :, :].rearrange("s k v -> k s v"),
                  writes=[("Ssm", sl)] + [("Ssm", sl, s_) for s_ in range(NS)])
        ssm_load(0)
        ssm_load(1)
        for h_ in range(4):
            sample_head(h_)
        for h in range(4):
            P.op("act", lambda e, h=h: e.activation(out=onb[0:NS, 0, h * 256:(h + 1) * 256], in_=PSF(h)[0:NS, 0:256], func=AF.Square,
                                                    accum_out=ssg[0:NS, h:h + 1]),
                 reads=psk(h), writes=[("on", 0, h), ("ssg", "ss", h)])
        P.op("act", lambda e: e.activation(out=ssg[0:NS, 8:12], in_=ssg[0:NS, 0:4], func=AF.Sqrt, scale=1.0 / 256, bias=1e-6),
             reads=[("ssg", "ss", ci) for ci in range(4)], writes=[("ssg", "sd")])
        P.op("dve", lambda e: e.reciprocal(out=ssg[0:NS, 16:20], in_=ssg[0:NS, 8:12]), reads=[("ssg", "sd")], writes=[("ssg", "rs")])
        for h in range(4):
            P.op("act", lambda e, h=h: e.activation(out=onb[0:NS, 0, h * 256:(h + 1) * 256], in_=PSF(h)[0:NS, 0:256], func=AF.Copy,
                                                    scale=ssg[0:NS, 16 + h:17 + h]),
                 reads=psk(h) + [("ssg", "rs")], writes=[("on", 0, h)])
        for vt in range(8):
            P.op("pe", lambda e, vt=vt: e.transpose(out=PSB(7)[:, vt * 16:(vt + 1) * 16], in_=onb[0:NS, 0, vt * 128:(vt + 1) * 128],
                                                    identity=identb[0:NS, 0:NS]),
                 reads=[("on", 0, vt // 2), "identb"], writes=psk(7), sig=(vt == 7))
        for vt in range(8):
            P.op("dve", lambda e, vt=vt: e.scalar_tensor_tensor(out=mixT[:, vt, NPR:NT], in0=PSB(7)[:, vt * 16:(vt + 1) * 16],
                                                                scalar=vecs[:, 4 + vt % 2:5 + vt % 2], in1=sg[:, vt, NPR:NT],
                                                                op0=ALU.mult, op1=ALU.mult),
                 reads=psk(7) + [gn_key, ("sg", vt, 2)], writes=[("mixT", "s", vt)])
    tap("mixs", mixT[:, 0:8, NPR:NT], [128, 8, NS])

    P.phase = 7
    fullg = A("fullg", BF16).rearrange("p (g t) -> p g t", g=8)
    glus = A("glus").rearrange("p (g s) -> p g s", g=8)
    glul = A("glul").rearrange("p (g s) -> p g s", g=8)
    if stage >= 6:
        sgm = A("sgm").rearrange("p (b t) -> p b t", b=2)
        gl32 = A("gl32").rearrange("p (b t) -> p b t", b=2)
        nblk = len(BLK_A)
        wT3 = wT.rearrange("p (g j) -> p g j", g=8)
        sctok = A("sctok")
        scT = A("scT").rearrange("p (g r) -> p g r", g=8)
        ys32 = A("ys32").rearrange("p (a g s) -> p a g s", a=3, g=8)
        otok = A("sctok")
        y32 = A("y32").rearrange("p (g t) -> p g t", g=8)
        ybf = A("ybf", BF16).rearrange("p (b t) -> p b t", b=4)
        ysq = A("ysq", BF16).rearrange("p (b t) -> p b t", b=4)
        lnt = A("lnt").rearrange("p (a t) -> p a t", a=4)
        lnd = A("lnd").rearrange("p (a t) -> p a t", a=2)
        Dm = A("Dm", BF16).rearrange("p (b c) -> p b c", b=16)
        P.dma("sp", "d_cw", sctok[0:31, :], conv_w[:, :], writes=["sctok"])
        for g in range(8):
            P.op("pe", lambda e, g=g: e.transpose(out=PSF(6)[:, g * 32:g * 32 + 31], in_=sctok[0:31, g * 128:(g + 1) * 128],
                                                  identity=identf[0:31, 0:31]),
                 reads=["sctok", "identf"], writes=psk(6), sig=(g == 7))
        P.op("dve", lambda e: e.tensor_copy(out=wT3[:, :, 0:31], in_=PSF(6)[:, 0:256].rearrange("p (g j) -> p g j", g=8)[:, :, 0:31]),
             reads=psk(6), writes=["wT"])


        def ln_block(yget, n, dst_col, s1b, s2b, blk_key):
            P.op("act", lambda e: e.activation(out=lnt[:, 0, 0:n], in_=PSF(s1b)[:, 0:n], func=AF.Copy, scale=1.0 / 1024),
                 reads=psk(s1b), writes=[("lnt", 0)])
            P.op("act", lambda e: e.activation(out=lnt[:, 1, 0:n], in_=PSF(s1b)[:, 0:n], func=AF.Square, scale=1.0 / 1024),
                 reads=psk(s1b), writes=[("lnt", 1)])
            P.op("dve", lambda e: e.scalar_tensor_tensor(out=lnt[:, 2, 0:n], in0=PSF(s2b)[:, 0:n], scalar=1.0 / 1024, in1=lnt[:, 1, 0:n],
                                                         op0=ALU.mult, op1=ALU.subtract),
                 reads=psk(s2b) + [("lnt", 1)], writes=[("lnt", 2)])
            P.op("act", lambda e: e.activation(out=lnt[:, 3, 0:n], in_=lnt[:, 2, 0:n], func=AF.Sqrt, bias=1e-5),
                 reads=[("lnt", 2)], writes=[("lnt", 3)])
            P.op("dve", lambda e: e.reciprocal(out=lnt[:, 1, 0:n], in_=lnt[:, 3, 0:n]), reads=[("lnt", 3)], writes=[("lnt", 1)])
            for g in range(8):
                sl = g % 2
                ysrc, ykeys = yget(g)
                P.op("dve", lambda e, sl=sl, ysrc=ysrc: e.tensor_tensor(out=lnd[:, sl, 0:n], in0=ysrc, in1=lnt[:, 0, 0:n], op=ALU.subtract),
                     reads=ykeys + [("lnt", 0)], writes=[("lnd", sl)])
                P.op("dve", lambda e, sl=sl: e.tensor_tensor(out=lnd[:, sl, 0:n], in0=lnd[:, sl, 0:n], in1=lnt[:, 1, 0:n], op=ALU.mult),
                     reads=[("lnd", sl), ("lnt", 1)], writes=[("lnd", sl)])
                P.op("act", lambda e, sl=sl, g=g: e.activation(out=mixT[:, 8 + g, dst_col:dst_col + n], in_=lnd[:, sl, 0:n], func=AF.Silu,
                                                               scale=vecs[:, 16 + g:17 + g], bias=vecs[:, 24 + g:25 + g]),
                     reads=[("lnd", sl), ("vecs", "lg"), ("vecs", "lb")], writes=[("mixT", 8 + g, blk_key)])


        def sample_round(rt):
            r = [128, 128, 128, 96][rt]
            P.dma("sp", "d_cw", sctok[0:r, :], sconv[rt * 128: rt * 128 + r, :], writes=["sctok"])
            for half in range(2):
                tb = 6 + half
                for gi in range(4):
                    g = half * 4 + gi
                    P.op("pe", lambda e, r=r, g=g, gi=gi, tb=tb: e.transpose(out=PSF(tb)[:, gi * 128: gi * 128 + r],
                                                                             in_=sctok[0:r, g * 128:(g + 1) * 128], identity=identf[0:r, 0:r]),
                         reads=["sctok", "identf"], writes=psk(tb), sig=(gi == 3))
                P.op("act" if half == 0 else "dve",
                     (lambda e, r=r, rt=rt, half=half, tb=tb: e.activation(
                         out=scT[:, half * 4:(half + 1) * 4, rt * 128: rt * 128 + r],
                         in_=PSF(tb)[:, :].rearrange("p (g t) -> p g t", g=4)[:, :, 0:r], func=AF.Copy)) if half == 0 else
                     (lambda e, r=r, rt=rt, half=half, tb=tb: e.tensor_copy(
                         out=scT[:, half * 4:(half + 1) * 4, rt * 128: rt * 128 + r],
                         in_=PSF(tb)[:, :].rearrange("p (g t) -> p g t", g=4)[:, :, 0:r])),
                     reads=psk(tb), writes=[("scT", rt, half)])

        def sample_finish():
            sck = [("scT", rt, half) for rt in range(4) for half in range(2)]
            scT4 = A("scT").rearrange("p (g s j) -> p g s j", g=8, s=16)
            for g in range(8):
                P.op("dve", lambda e, g=g: e.tensor_tensor(out=scT4[:, g, :, :], in0=scT4[:, g, :, :],
                                                           in1=wT3[:, g, 0:30].rearrange("p (o j) -> p o j", o=1).broadcast_to([128, 16, 30]),
                                                           op=ALU.mult),
                     reads=sck + ["wT"], writes=[("scTp", g)])
            P.op("dve", lambda e: e.tensor_reduce(out=ys32[:, 0, :, :].rearrange("p g s -> p (g s)"),
                                                  in_=A("scT").rearrange("p (a j) -> p a j", j=30), axis=AX.X, op=ALU.add),
                 reads=[("scTp", g) for g in range(8)], writes=[("ys32", 0)])
            P.op("dve", lambda e: e.tensor_tensor(out=ys32[:, 1, :, :], in0=glus, in1=wT3[:, :, 30:31].broadcast_to([128, 8, 16]), op=ALU.mult),
                 reads=[("glus", g) for g in range(8)] + ["wT"], writes=[("ys32", 1)])
            P.op("dve", lambda e: e.tensor_tensor(out=ys32[:, 0, :, :], in0=ys32[:, 0, :, :], in1=ys32[:, 1, :, :], op=ALU.add),
                 reads=[("ys32", 0), ("ys32", 1)], writes=[("ys32", 0)])
            P.op("dve", lambda e: e.tensor_tensor(out=ys32[:, 0, :, :], in0=ys32[:, 0, :, :],
                                                  in1=vecs[:, 8:16].rearrange("p (g o) -> p g o", o=1).broadcast_to([128, 8, 16]), op=ALU.add),
                 reads=[("ys32", 0), ("vecs", "cb")], writes=[("ys32", 0)])
            ysb = A("ybf", BF16)[:, 0:128].rearrange("p (g s) -> p g s", g=8)
            ysqb = A("ysq", BF16)[:, 0:128].rearrange("p (g s) -> p g s", g=8)
            P.op("act", lambda e: e.activation(out=ysb, in_=ys32[:, 0, :, :], func=AF.Copy), reads=[("ys32", 0)], writes=[("ybf", 0)])
            P.op("act", lambda e: e.activation(out=ysqb, in_=ys32[:, 0, :, :], func=AF.Square), reads=[("ys32", 0)], writes=[("ysq", 0)])
            for g in range(8):
                P.op("pe", lambda e, g=g: e.matmul(PSF(4)[:, 0:NS], lhsT=onesb, rhs=ysb[:, g, :], start=(g == 0), stop=(g == 7)),
                     reads=["onesb", ("ybf", 0)], writes=psk(4), sig=(g == 7))
            for g in range(8):
                P.op("pe", lambda e, g=g: e.matmul(PSF(5)[:, 0:NS], lhsT=onesb, rhs=ysqb[:, g, :], start=(g == 0), stop=(g == 7)),
                     reads=["onesb", ("ysq", 0)], writes=psk(5), sig=(g == 7))
            ln_block(lambda g: (ys32[:, 0, g, :], [("ys32", 0)]), NS, NPR, 4, 5, "s")


            for half in range(2):
                tb = 6 + half
                for gi in range(4):
                    g = half * 4 + gi
                    P.op("pe", lambda e, g=g, gi=gi, tb=tb: e.transpose(out=PSF(tb)[0:30, gi * 128:(gi + 1) * 128], in_=glul[:, g, 0:30], identity=identf),
                         reads=[("glul", g), "identf"], writes=psk(tb), sig=(gi == 3))
                P.op("act", lambda e, half=half, tb=tb: e.activation(out=otok[0:30, half * 512:(half + 1) * 512], in_=PSF(tb)[0:30, :], func=AF.Copy),
                     reads=psk(tb), writes=["sctok"])
            P.dma("sp", "d_convp", convp[:, :], otok[0:30, :], reads=["sctok"], final=True)

            convs3 = convs.rearrange("(s j) c -> s j c", j=30)
            sconv3 = sconv.rearrange("(s j) c -> s j c", j=30)
            P.dma("sp", "d_convs0", convs3[:, 0:29, :], sconv3[:, 1:30, :], final=True)
            for half in range(2):
                tb = 6 + half
                for gi in range(4):
                    g = half * 4 + gi
                    P.op("pe", lambda e, g=g, gi=gi, tb=tb: e.transpose(out=PSF(tb)[0:NS, gi * 128:(gi + 1) * 128], in_=glus[:, g, :], identity=identf),
                         reads=[("glus", g), "identf"], writes=psk(tb), sig=(gi == 3))
                P.op("act", lambda e, half=half, tb=tb: e.activation(out=otok[0:NS, half * 512:(half + 1) * 512], in_=PSF(tb)[0:NS, :], func=AF.Copy),
                     reads=psk(tb), writes=["sctok"])
            P.dma("sp", "d_convs1", convs3[:, 29, :], otok[0:NS, :], reads=["sctok"], final=True)


        def conv_pair(g0):
            def ug_cons_f(g):
                def ug_cons(bi, b, b0, n):
                    P.op("act", lambda e: e.activation(out=sgm[:, g % 2, b0:b0 + n], in_=PSF(b)[:, 0:n], func=AF.Sigmoid),
                         reads=psk(b), writes=[("sgm", g % 2, bi)])
                return ug_cons

            def ua_cons_f(g):
                gs = g % 2

                def ua_cons(bi, b, b0, n):
                    P.op("dve", lambda e: e.tensor_tensor(out=gl32[:, gs, b0:b0 + n], in0=PSF(b)[:, 0:n], in1=sgm[:, gs, b0:b0 + n], op=ALU.mult),
                         reads=psk(b) + [("sgm", gs, bi)], writes=[("gl32", gs, bi)])
                return ua_cons
            proj2(w_in, C_UG + g0 * 128, hT, main_src, BLK_A, [ug_cons_f(g0), ug_cons_f(g0 + 1)], accA)
            proj2(w_in, C_UA + g0 * 128, hT, main_src, BLK_A, [ua_cons_f(g0), ua_cons_f(g0 + 1)], accA)
            for g in (g0, g0 + 1):
                gs = g % 2
                allk = [("gl32", gs, bi) for bi in range(nblk)]
                P.op("act", lambda e, g=g, gs=gs: e.activation(out=fullg[:, g, 30:30 + NPR], in_=gl32[:, gs, 0:NPR], func=AF.Copy),
                     reads=allk, writes=[("fullg", g, "m")])
                P.op("dve", lambda e, g=g, gs=gs: e.tensor_copy(out=fullg[:, g, 0:30], in_=gl32[:, gs, NTA - 30:NTA]),
                     reads=allk, writes=[("fullg", g, "h")])
                P.op("dve", lambda e, g=g, gs=gs: e.tensor_copy(out=glus[:, g, :], in_=gl32[:, gs, NPR:NT]), reads=allk, writes=[("glus", g)])
                P.op("dve", lambda e, g=g, gs=gs: e.tensor_copy(out=glul[:, g, 0:30], in_=gl32[:, gs, NPR - 30:NPR]), reads=allk, writes=[("glul", g)])
        for gi_, g_ in enumerate(range(0, 8, 2)):
            conv_pair(g_)
            sample_round(gi_)
        sample_finish()

    P.phase = 8
    Wo = [A(n_, BF16).rearrange("p (k c) -> p k c", k=KC) for n_ in ("WoA", "WoB", "WoC", "WoD")]
    Wo_names = ("WoA", "WoB", "WoC", "WoD")
    if stage >= 7:
        for cbk in (0, 1, 2):
            P.dma("pool", f"d_wo{cbk}", Wo[cbk], w_out[:, cbk * 512:(cbk + 1) * 512].rearrange("(k p) c -> p k c", p=128),
                  writes=[Wo_names[cbk]])
        cvt_dmas()
        d_rr = [0]

        def conv_all():
            s1 = [4, 5]
            s2 = [6, 7]

            def conv_mm(g):
                cb = [(g % 2) * 2, (g % 2) * 2 + 1]
                for j in range(31):
                    ds = d_rr[0] % 16
                    d_rr[0] += 1
                    if j % 3 != 2:
                        P.op("dve", lambda e, ds=ds, g=g, j=j: e.tensor_scalar(out=Dm[:, ds, :], in0=identf, scalar1=wT3[:, g, j:j + 1],
                                                                               scalar2=None, op0=ALU.mult),
                             reads=["identf", "wT"], writes=[("Dm", ds)])
                    else:
                        P.op("act", lambda e, ds=ds, g=g, j=j: e.activation(out=Dm[:, ds, :], in_=identf, func=AF.Copy, scale=wT3[:, g, j:j + 1]),
                             reads=["identf", "wT"], writes=[("Dm", ds)])
                    for pb in range(2):
                        P.op("pe", lambda e, ds=ds, g=g, j=j, pb=pb, cbank=cb[pb]: e.matmul(
                            PSF(cbank)[:, 0:512], lhsT=Dm[:, ds, :], rhs=fullg[:, g, pb * 512 + j: pb * 512 + j + 512],
                            start=(j == 0), stop=(j == 30)),
                            reads=[("Dm", ds), ("fullg", g, "m"), ("fullg", g, "h")], writes=psk(cb[pb]))

            def evac_stats(g):
                cb = [(g % 2) * 2, (g % 2) * 2 + 1]
                for pb in range(2):
                    cbank = cb[pb]
                    sl = (g * 2 + pb) % 4
                    P.op("act", lambda e, g=g, cbank=cbank, pb=pb: e.activation(out=y32[:, g, pb * 512:(pb + 1) * 512], in_=PSF(cbank)[:, 0:512],
                                                                                func=AF.Identity, bias=vecs[:, 8 + g:9 + g]),
                         reads=psk(cbank) + [("vecs", "cb")], writes=[("y32", g, pb)])
                    P.op("dve", lambda e, g=g, sl=sl, pb=pb: e.tensor_tensor(out=ysq[:, sl, :], in0=y32[:, g, pb * 512:(pb + 1) * 512],
                                                                             in1=y32[:, g, pb * 512:(pb + 1) * 512], op=ALU.mult),
                         reads=[("y32", g, pb)], writes=[("ysq", sl)])
                    P.op("dve", lambda e, g=g, sl=sl, pb=pb: e.tensor_copy(out=ybf[:, sl, :], in_=y32[:, g, pb * 512:(pb + 1) * 512]),
                         reads=[("y32", g, pb)], writes=[("ybf", sl)])
                    P.op("pe", lambda e, g=g, sl=sl, pb=pb: e.matmul(PSF(s1[pb])[:, 0:512], lhsT=onesb, rhs=ybf[:, sl, :],
                                                                     start=(g == 0), stop=(g == 7)),
                         reads=["onesb", ("ybf", sl)], writes=psk(s1[pb]))
                    P.op("pe", lambda e, g=g, sl=sl, pb=pb: e.matmul(PSF(s2[pb])[:, 0:512], lhsT=onesb, rhs=ysq[:, sl, :],
                                                                     start=(g == 0), stop=(g == 7)),
                         reads=["onesb", ("ysq", sl)], writes=psk(s2[pb]))
            for g in range(8):
                conv_mm(g)
                if g >= 1:
                    evac_stats(g - 1)
            evac_stats(7)
            for pb in range(2):
                ln_block((lambda g, pb=pb: (y32[:, g, pb * 512:(pb + 1) * 512], [("y32", g, pb)])), 512, pb * 512, s1[pb], s2[pb], pb)

        conv_all()
    tap("mixc", mixT[:, 8:16, 0:NT], [128, 8, NT])

    P.phase = 9
    h2T = A("h2T", BF16)
    h2T3 = h2T.rearrange("p (k t) -> p k t", k=KC)
    tok_tiles = [(t, t * 128, 128) for t in range(8)] + [(8, NPR, NS)]
    if stage >= 8:
        for cbk in (3,):
            P.dma("pool", f"d_wo{cbk}", Wo[cbk], w_out[:, cbk * 512:(cbk + 1) * 512].rearrange("(k p) c -> p k c", p=128),
                  writes=[Wo_names[cbk]])
        load_gain(norm_ffn, "d_gbB", "gbB")
        accB = [0, 1, 2, 3, 4, 5]

        def mix_keys(t):
            if t < 8:
                return [("mixT", kc, t // 4) for kc in range(KC)]
            return [("mixT", "s", vt) for vt in range(8)] + [("mixT", 8 + g, "s") for g in range(8)]

        def resid(i, xsl, kx, r):
            t, c0, _ = tok_tiles[i]
            src = xp[c0:c0 + r, :] if t < 8 else xsm[:, :]
            P.dma("sp", f"d_xsB{i % 2}", xsl[0:r, :], src, writes=[kx])
            for cbk in range(4):
                b = next_bank(accB)
                for kc in range(KC):
                    P.op("pe", lambda e, b=b, kc=kc, cbk=cbk: e.matmul(PSF(b)[0:r, 0:512], lhsT=mixT[:, kc, c0:c0 + r], rhs=Wo[cbk][:, kc, :],
                                                                       start=(kc == 0), stop=(kc == KC - 1)),
                         reads=mix_keys(t) + [Wo_names[cbk]], writes=psk(b), sig=(kc == KC - 1))
                P.op("dve", lambda e, b=b, cbk=cbk: e.tensor_tensor(out=xsl[0:r, cbk * 512:(cbk + 1) * 512], in0=PSF(b)[0:r, 0:512],
                                                                    in1=xsl[0:r, cbk * 512:(cbk + 1) * 512], op=ALU.add),
                     reads=psk(b) + [kx], writes=[kx])

        def store_x1(i, xsl, kx, r):
            t, c0, _ = tok_tiles[i]
            P.dma("sp", f"d_x1st{i % 2}", x1d[c0:c0 + r, :], xsl[0:r, :], reads=[kx], writes=[("x1d", i)])

        b_tiles = [(None, r, c0) for (t, c0, r) in tok_tiles]
        norm_transpose(b_tiles, "xsB", "xnB", h2T, "h2T", NT, (6, 7), store_x1=store_x1, resid=resid, gname="gbB")
    tap("h2T", h2T, [128, KC * NT])

    P.phase = 10
    actT = A("actT", BF16).rearrange("p (j t) -> p j t", j=NJ)
    if stage >= 9:
        sgt = A("sgt").rearrange("p (b t) -> p b t", b=6)
        h2_src = tiles_of("h2T", [(t, c0, c0 + r) for (t, c0, r) in tok_tiles])
        accC = [0, 1, 2, 3, 4, 5, 6, 7]

        def ffn_pair(j0):
            def gate_cons_f(j):
                def gate_cons(bi, b, b0, n):
                    P.op("act", lambda e: e.activation(out=sgt[:, (j % 2) * 3 + bi, 0:n], in_=PSF(b)[:, 0:n], func=AF.Silu),
                         reads=psk(b), writes=[("sgt", (j % 2) * 3 + bi)])
                return gate_cons

            def up_cons_f(j):
                def up_cons(bi, b, b0, n):
                    P.op("dve", lambda e: e.tensor_tensor(out=actT[:, j, b0:b0 + n], in0=PSF(b)[:, 0:n], in1=sgt[:, (j % 2) * 3 + bi, 0:n], op=ALU.mult),
                         reads=psk(b) + [("sgt", (j % 2) * 3 + bi)], writes=[("actT", j, bi)])
                return up_cons
            proj2(w_ffn_in, j0 * 128, h2T, h2_src, BLK_C, [gate_cons_f(j0), gate_cons_f(j0 + 1)], accC, wkey="wsC")
            proj2(w_ffn_in, DFF + j0 * 128, h2T, h2_src, BLK_C, [up_cons_f(j0), up_cons_f(j0 + 1)], accC, wkey="wsC")
        for j_ in range(0, NJ, 2):
            ffn_pair(j_)

    P.phase = 11
    if stage >= 10:
        x2 = A("x2").rearrange("p (a c) -> p a c", a=5)
        yst = A("yst").rearrange("p (b t) -> p b t", b=4)
        junk = A("junkD", BF16)
        wsD = A("wsD", BF16).rearrange("p (b j c) -> p b j c", b=4, j=11)
        load_gain(norm_final, "d_gbD", "gbD")
        gbD = A("gbD")
        OBS = [dict(tiles=[0, 1, 2, 3], c0=0, chunks=[(0, 512)]),
               dict(tiles=[4, 5, 6, 7, 8], c0=512, chunks=[(512, 264), (776, 264)])]
        accD = [0, 1, 2, 3, 4, 5]
        wd_rr = [0]

        def do_transposes(ob, m, ysl, nch):
            tl = ob["tiles"]
            ykeys = [("yst", ysl, ci) for ci in range(nch)]
            for li, t in enumerate(tl):
                _, c0, r = tok_tiles[t]
                tb = 6 if li < 4 else 7
                lo = (li % 4) * 128
                off = c0 - ob["c0"]
                last = (li == min(3, len(tl) - 1)) or (li == len(tl) - 1)
                P.op("pe", lambda e, r=r, tb=tb, lo=lo, off=off: e.transpose(out=PSF(tb)[0:r, lo:lo + 128], in_=yst[:, ysl, off:off + r], identity=identf),
                     reads=ykeys + ["identf"], writes=psk(tb), sig=last)
            nfull = min(4, len(tl))
            P.op("dve", lambda e: e.tensor_tensor(out=x2[:, 0:nfull, m * 128:(m + 1) * 128],
                                                  in0=PSF(6)[:, 0:nfull * 128].rearrange("p (a c) -> p a c", a=nfull),
                                                  in1=x2[:, 0:nfull, m * 128:(m + 1) * 128], op=ALU.add),
                 reads=psk(6) + [("x2", li) for li in range(nfull)], writes=[("x2", li) for li in range(nfull)])
            if len(tl) > 4:
                P.op("dve", lambda e: e.tensor_tensor(out=x2[0:NS, 4, m * 128:(m + 1) * 128], in0=PSF(7)[0:NS, 0:128],
                                                      in1=x2[0:NS, 4, m * 128:(m + 1) * 128], op=ALU.add),
                     reads=psk(7) + [("x2", 4)], writes=[("x2", 4)])

        def ffn_out_block(ob):
            tl = ob["tiles"]
            for li, t in enumerate(tl):
                _, c0, r = tok_tiles[t]
                P.dma("sp", f"d_x2l{li}", x2[0:r, li, :], x1d[c0:c0 + r, :], reads=[("x1d", t)], writes=[("x2", li)])
            pending = []
            chunks = ob["chunks"]
            nch = len(chunks)
            for mp in range(8):
                bk = [[next_bank(accD) for ci in range(nch)] for mi in range(2)]
                for kq in range(4):
                    slot = wd_rr[0] % 4
                    wd_rr[0] += 1
                    if mp < NCVT:
                        P.dma("pool", f"d_wsD{slot}", wsD[:, slot, :, :], wbf[mp, kq].rearrange("p (j c) -> p j c", j=11),
                              reads=[("wbf", mp, kq)], writes=[("wsD", slot)])
                    else:
                        P.dma("pool", f"d_wsD{slot}", wsD[:, slot, :, :],
                              w_ffn_out[kq * 1408:(kq + 1) * 1408, mp * 256:(mp + 1) * 256].rearrange("(j p) c -> p j c", p=128),
                              writes=[("wsD", slot)])
                    for mi in range(2):
                        for ci, (c0, n) in enumerate(chunks):
                            b = bk[mi][ci]
                            for jj in range(11):
                                j = kq * 11 + jj
                                P.op("pe", lambda e, b=b, j=j, jj=jj, c0=c0, n=n, slot=slot, mi=mi: e.matmul(
                                    PSF(b)[:, 0:n], lhsT=wsD[:, slot, jj, mi * 128:(mi + 1) * 128], rhs=actT[:, j, c0:c0 + n],
                                    start=(j == 0), stop=(j == NJ - 1)),
                                    reads=[("wsD", slot)] + [("actT", j, bi) for bi in range(3)], writes=psk(b), sig=(jj == 10))
                cur = []
                for mi in range(2):
                    m = mp * 2 + mi
                    ysl = m % 4
                    for ci, (c0, n) in enumerate(chunks):
                        b = bk[mi][ci]
                        off = c0 - ob["c0"]
                        P.op("act", lambda e, b=b, n=n, off=off, ysl=ysl: e.activation(out=yst[:, ysl, off:off + n], in_=PSF(b)[:, 0:n], func=AF.Copy),
                             reads=psk(b), writes=[("yst", ysl, ci)])
                    cur.append((m, ysl))
                for (m, ysl) in pending:
                    do_transposes(ob, m, ysl, nch)
                pending = cur
            for (m, ysl) in pending:
                do_transposes(ob, m, ysl, nch)
            for li, t in enumerate(tl):
                _, c0, r = tok_tiles[t]
                s4 = 8 + 4 * (li % 2)
                P.op("act", lambda e, r=r, li=li, s4=s4: e.activation(out=junk[0:r, :], in_=x2[0:r, li, :], func=AF.Square,
                                                                      accum_out=small[0:r, s4:s4 + 1]),
                     reads=[("x2", li)], writes=["junkD", ("small", "f", li % 2)])
                P.op("act", lambda e, r=r, s4=s4: e.activation(out=small[0:r, s4 + 1:s4 + 2], in_=small[0:r, s4:s4 + 1], func=AF.Sqrt,
                                                               scale=1.0 / D, bias=1e-6),
                     reads=[("small", "f", li % 2)], writes=[("small", "f", li % 2)])
                P.op("dve", lambda e, r=r, s4=s4: e.reciprocal(out=small[0:r, s4 + 2:s4 + 3], in_=small[0:r, s4 + 1:s4 + 2]),
                     reads=[("small", "f", li % 2)], writes=[("small", "f", li % 2)])
                P.op("dve", lambda e, r=r, li=li, s4=s4: e.scalar_tensor_tensor(out=x2[0:r, li, :], in0=x2[0:r, li, :],
                                                                               scalar=small[0:r, s4 + 2:s4 + 3], in1=gbD[0:r, :],
                                                                               op0=ALU.mult, op1=ALU.mult),
                     reads=[("x2", li), ("small", "f", li % 2), "gbD"], writes=[("x2", li)])
                dst = yp[c0:c0 + r, :] if t < 8 else ys[:, :]
                P.dma("sp", f"d_yout{li}", dst, x2[0:r, li, :], reads=[("x2", li)], final=True)
        for ob_ in OBS:
            ffn_out_block(ob_)

    P.wait_tokens("sp", P.final_tokens)
    return nc, P, st, dict(arena=arena, A=A, tap_out=tap_out, top=top, tap=tap)


def _consts():
    ident = np.eye(128, dtype=np.float32)
    mask = np.triu(np.ones((128, 128), dtype=np.float32))
    reset = np.ones((128, NTA), dtype=np.float32)
    reset[:, 0:NPR:128] = 0.0
    reset[:, NPR:] = 0.0
    delta = np.tile(np.eye(16, dtype=np.float32).reshape(1, 256), (128, 1))
    return dict(c_ident=ident, c_mask=mask, c_reset=reset, c_delta=delta)


def make_in_maps(inp):
    f = lambda a: np.ascontiguousarray(np.asarray(a, dtype=np.float32))
    xprompt = f(inp["x_prompt"]); xsample = f(inp["x_sample"])
    sg_ = f(inp["state_gla"])[0]; sc_ = f(inp["state_conv"])[0]
    shared = dict(
        norm_mix=f(inp["norm_mix"])[0], w_in=f(inp["w_in"])[0], w_gate_up=f(inp["w_gate_up"])[0], b_gate=f(inp["b_gate"])[0],
        gla_norm=f(inp["gla_norm"])[0], conv_w=f(inp["conv_w"])[0], conv_b=f(inp["conv_b"])[0], conv_ln_g=f(inp["conv_ln_g"])[0],
        conv_ln_b=f(inp["conv_ln_b"])[0], w_out=f(inp["w_out"])[0], norm_ffn=f(inp["norm_ffn"])[0], w_ffn_in=f(inp["w_ffn_in"])[0],
        w_ffn_out=f(inp["w_ffn_out"])[0], norm_final=f(inp["norm_final"]))
    shared.update(_consts())
    zeros_pre = np.zeros((NPRE, D), dtype=np.float32)
    maps = []
    for c in range(8):
        s, hf = c // 2, c % 2
        m = dict(shared)
        m["xp"] = np.ascontiguousarray(xprompt[s, hf * NPR:(hf + 1) * NPR])
        m["xpre"] = np.ascontiguousarray(xprompt[s, 0:NPRE]) if hf == 1 else zeros_pre
        m["xsm"] = np.ascontiguousarray(xsample[c * NS:(c + 1) * NS, 0])
        m["sgla"] = np.ascontiguousarray(sg_[c * NS:(c + 1) * NS])
        m["sconv"] = np.ascontiguousarray(sc_[c * NS:(c + 1) * NS].reshape(NS * 30, 1024))
        maps.append(m)
    return maps


_CACHE = {}


def kernel(**inputs):
    if "prog" not in _CACHE:
        nc, P, st, info = build_program()
        P.emit()
        _CACHE["prog"] = (nc, P, st)
    nc = _CACHE["prog"][0]
    maps = make_in_maps(inputs)
    res = run_bass_kernel_spmd(nc, maps, core_ids=list(range(8)))
    R = res.results
    y_prompt = np.zeros((4, 2048, D), np.float32)
    y_sample = np.zeros((128, 1, D), np.float32)
    gla_p = np.zeros((1, 4, 4, 128, 256), np.float32)
    conv_p = np.zeros((1, 4, 30, 1024), np.float32)
    gla_s = np.zeros((1, 128, 4, 128, 256), np.float32)
    conv_s = np.zeros((1, 128, 30, 1024), np.float32)
    for c in range(8):
        s, hf = c // 2, c % 2
        y_prompt[s, hf * NPR:(hf + 1) * NPR] = R[c]["yp"]
        y_sample[c * NS:(c + 1) * NS, 0] = R[c]["ys"]
        gla_s[0, c * NS:(c + 1) * NS] = R[c]["glas"]
        conv_s[0, c * NS:(c + 1) * NS] = np.asarray(R[c]["convs"]).reshape(NS, 30, 1024)
        if hf == 1:
            gla_p[0, s] = R[c]["glap"]
            conv_p[0, s] = R[c]["convp"]
    return (y_prompt, y_sample, gla_p, conv_p, gla_s, conv_s)
```
